# Optimizing a Trainium2 kernel written in Bass

```python
import math
import jax, jax.numpy as jnp
from jax import lax
import numpy as np

D_MODEL = 2048
BATCH = 4
SEQ = 2048
DEPTH = 1

HEAD_DIM = 128
SB_HEADS = 8
DIL_GROUPS = ((128, 1), (512, 4), (2048, 16))
DIL_HEADS_PER_GROUP = 4
N_DIL_GROUPS = len(DIL_GROUPS)
DIL_HEADS = DIL_HEADS_PER_GROUP * N_DIL_GROUPS
BLOCK = 128
SB_W = SB_HEADS * HEAD_DIM
DIL_W = DIL_HEADS * HEAD_DIM
DIL_OUT = DIL_HEADS_PER_GROUP * HEAD_DIM
N_BRANCHES = 2
IN_COLS = 3 * SB_W + 3 * DIL_W + N_BRANCHES * D_MODEL
D_FF = -(-8 * D_MODEL // (3 * 256)) * 256
N_MOD = 6
EPS = 1e-6

kernel_name = "hybrid_stickbreak_dilated_alibi_swiglu_adaln"


def rmsnorm(x, g):
    xf = x.astype(jnp.float32)
    y = xf * lax.rsqrt(jnp.mean(xf * xf, axis=-1, keepdims=True) + EPS)
    return (y * g.astype(jnp.float32)).astype(x.dtype)


def alibi_slopes():
    return 2.0 ** (-8.0 * (jnp.arange(DIL_HEADS, dtype=jnp.float32) + 1.0) / DIL_HEADS)


def stick_breaking_attention(q, k, v):
    B, S, H, Dh = q.shape
    nb = S // BLOCK
    scale = Dh ** -0.5
    qb = q.reshape(B, nb, BLOCK, H, Dh).transpose(1, 0, 2, 3, 4)
    key_pos = jnp.arange(S)

    def one_block(args):
        qi, blk = args
        z = jnp.einsum('bqhd,bkhd->bhqk', qi, k, preferred_element_type=jnp.float32) * scale
        q_pos = blk * BLOCK + jnp.arange(BLOCK)
        mask = key_pos[None, :] < q_pos[:, None]
        log_1m = jnp.where(mask, -jax.nn.softplus(z), 0.0)
        suffix = lax.cumsum(log_1m, axis=3, reverse=True) - log_1m
        a = jnp.where(mask, jnp.exp(jax.nn.log_sigmoid(z) + suffix), 0.0)
        return jnp.einsum('bhqk,bkhd->bqhd', a.astype(v.dtype), v)

    out = lax.map(one_block, (qb, jnp.arange(nb)))
    return out.transpose(1, 0, 2, 3, 4).reshape(B, S, H, Dh)


def dilated_window_attention(q, k, v, slopes, window, dilation):
    B, S, H, Dh = q.shape
    span = window // dilation
    L = S // dilation
    Lp = -(-L // BLOCK) * BLOCK
    nb = Lp // BLOCK
    scale = Dh ** -0.5

    def to_sub(t):
        t = t.reshape(B, L, dilation, H, Dh).transpose(0, 2, 1, 3, 4)
        t = jnp.pad(t, ((0, 0), (0, 0), (0, Lp - L), (0, 0), (0, 0)))
        return t.reshape(B, dilation, nb, BLOCK, H, Dh)

    def with_prev(t):
        prev = jnp.pad(t, ((0, 0), (0, 0), (1, 0), (0, 0), (0, 0), (0, 0)))[:, :, :-1]
        return jnp.concatenate([prev, t], axis=3)

    qs = to_sub(q)
    kk = with_prev(to_sub(k))
    vv = with_prev(to_sub(v))
    s = jnp.einsum('brnqhd,brnkhd->brnhqk', qs, kk, preferred_element_type=jnp.float32) * scale
    a_idx = jnp.arange(BLOCK)
    b_idx = jnp.arange(2 * BLOCK)
    rel = a_idx[:, None] + BLOCK - b_idx[None, :]
    blk = jnp.arange(nb)
    valid = ((rel >= 0) & (rel <= span))[None] & ((blk[:, None, None] > 0) | (b_idx[None, None, :] >= BLOCK))
    bias = -slopes[:, None, None] * (dilation * rel).astype(jnp.float32)[None]
    s = jnp.where(valid[None, None, :, None], s + bias[None, None, None], -jnp.inf)
    m = jnp.max(s, axis=-1, keepdims=True)
    p = jnp.exp(s - m)
    l = jnp.sum(p, axis=-1, keepdims=True)
    o = jnp.einsum('brnhqk,brnkhd->brnqhd', (p / l).astype(v.dtype), vv)
    lse = (m + jnp.log(l))[..., 0]
    o = o.reshape(B, dilation, Lp, H, Dh)[:, :, :L].transpose(0, 2, 1, 3, 4).reshape(B, S, H, Dh)
    lse = lse.transpose(0, 1, 2, 4, 3).reshape(B, dilation, Lp, H)[:, :, :L]
    lse = lse.transpose(0, 2, 1, 3).reshape(B, S, H)
    return o, lse


def setup_inputs(seed: int = 0) -> dict:
    key = jax.random.key(seed)
    ks = jax.random.split(key, 16)
    f32 = jnp.float32
    nrm = lambda k, shape, fan_in, gain=1.0: jax.random.normal(k, shape, f32) * (gain * fan_in ** -0.5)
    return {
        "x": jax.random.normal(ks[0], (BATCH, SEQ, D_MODEL), f32),
        "c": jax.random.normal(ks[1], (BATCH, D_MODEL), f32),
        "w_ada": nrm(ks[2], (DEPTH, D_MODEL, N_MOD * D_MODEL), D_MODEL, 0.5),
        "b_ada": 0.01 * jax.random.normal(ks[3], (DEPTH, N_MOD * D_MODEL), f32),
        "g_norm1": 1.0 + 0.01 * jax.random.normal(ks[4], (DEPTH, D_MODEL), f32),
        "g_norm2": 1.0 + 0.01 * jax.random.normal(ks[5], (DEPTH, D_MODEL), f32),
        "g_final": 1.0 + 0.01 * jax.random.normal(ks[6], (D_MODEL,), f32),
        "w_in": nrm(ks[7], (DEPTH, D_MODEL, IN_COLS), D_MODEL),
        "b_gate": 0.01 * jax.random.normal(ks[8], (DEPTH, N_BRANCHES * D_MODEL), f32),
        "w_proj_sb": nrm(ks[9], (DEPTH, SB_W, D_MODEL), SB_W),
        "w_proj_dil": nrm(ks[10], (DEPTH, DIL_OUT, D_MODEL), DIL_OUT),
        "w_out": nrm(ks[11], (DEPTH, D_MODEL, D_MODEL), D_MODEL),
        "w_ffn_gate": nrm(ks[12], (DEPTH, D_MODEL, D_FF), D_MODEL),
        "w_ffn_up": nrm(ks[13], (DEPTH, D_MODEL, D_FF), D_MODEL),
        "w_ffn_down": nrm(ks[14], (DEPTH, D_FF, D_MODEL), D_FF),
    }


def reference(x, c, w_ada, b_ada, g_norm1, g_norm2, g_final, w_in, b_gate, w_proj_sb, w_proj_dil,
              w_out, w_ffn_gate, w_ffn_up, w_ffn_down):
    B, S, D = x.shape
    slopes = alibi_slopes().reshape(N_DIL_GROUPS, DIL_HEADS_PER_GROUP)
    split_idx = [int(i) for i in np.cumsum([SB_W] * 3 + [DIL_W] * 3)]
    for l in range(DEPTH):
        mod = jax.nn.silu(c) @ w_ada[l] + b_ada[l]
        shift1, scale1, gate1, shift2, scale2, gate2 = [t[:, None, :] for t in jnp.split(mod, N_MOD, axis=-1)]

        h = rmsnorm(x, g_norm1[l]) * (1.0 + scale1) + shift1
        proj = h @ w_in[l]
        q_sb, k_sb, v_sb, q_dl, k_dl, v_dl, gate_pre = jnp.split(proj, split_idx, axis=-1)

        sb_shape = (B, S, SB_HEADS, HEAD_DIM)
        o_sb = stick_breaking_attention(q_sb.reshape(sb_shape), k_sb.reshape(sb_shape), v_sb.reshape(sb_shape))
        y_sb = o_sb.reshape(B, S, SB_W) @ w_proj_sb[l]

        dl_shape = (B, S, N_DIL_GROUPS, DIL_HEADS_PER_GROUP, HEAD_DIM)
        q_dl, k_dl, v_dl = q_dl.reshape(dl_shape), k_dl.reshape(dl_shape), v_dl.reshape(dl_shape)
        outs, lses = [], []
        for g, (window, dilation) in enumerate(DIL_GROUPS):
            o_g, lse_g = dilated_window_attention(q_dl[:, :, g], k_dl[:, :, g], v_dl[:, :, g],
                                                  slopes[g], window, dilation)
            outs.append(o_g)
            lses.append(lse_g)
        w_mix = jax.nn.softmax(jnp.stack(lses, axis=0), axis=0)
        o_dl = jnp.sum(w_mix[..., None].astype(x.dtype) * jnp.stack(outs, axis=0), axis=0)
        y_dl = o_dl.reshape(B, S, DIL_OUT) @ w_proj_dil[l]

        gates = jax.nn.sigmoid(gate_pre + b_gate[l])
        g_sb, g_dl = jnp.split(gates, N_BRANCHES, axis=-1)
        mixed = (g_sb * y_sb + g_dl * y_dl) @ w_out[l]
        x = x + gate1 * mixed

        h2 = rmsnorm(x, g_norm2[l]) * (1.0 + scale2) + shift2
        ffn = (jax.nn.silu(h2 @ w_ffn_gate[l]) * (h2 @ w_ffn_up[l])) @ w_ffn_down[l]
        x = x + gate2 * ffn
    return rmsnorm(x, g_final)
```

```python
import numpy as np
from contextlib import ExitStack
import concourse.bass as bass
import concourse.mybir as mybir
from concourse.bass_utils import run_bass_kernel_spmd

F32 = mybir.dt.float32
BF16 = mybir.dt.bfloat16
AF = mybir.ActivationFunctionType
ALU = mybir.AluOpType

D = 2048
S = 2048
OWN = 1024
NCH = 16
DFF = 5632
NFF = DFF // 128
EPS = 1e-6
QS = 128 ** -0.5
DEBUG = False
STOP = 99


class StopBuild(Exception):
    pass


class T:
    def __init__(self, ap=None, name=""):
        self.ap = ap
        self.w = {}
        self.r = {}
        self.name = name
        self.dkey = None
        self.dcnt = 0
        self.excl = False


class Prog:
    ENG = ("pe", "act", "dve", "pool", "sp")

    def __init__(self, nc, es):
        self.nc = nc
        self.es = es
        self.q = {e: [] for e in self.ENG}
        self.sems = {}
        self.cnt = {}
        self.seen = {e: {} for e in self.ENG}
        for e in self.ENG:
            self.sems[e] = es.enter_context(nc.semaphore("s_" + e))
            self.cnt[e] = 0
        self.nd = 0
        self.rr = 0
        self.log = {e: [] for e in self.ENG}

    def dma_sem(self, t):
        if t.dkey is None:
            t.dkey = "d%d" % self.nd
            self.nd += 1
            self.sems[t.dkey] = self.es.enter_context(self.nc.semaphore(t.dkey))
        return t.dkey

    def _waits(self, e, deps):
        for key, val in deps.items():
            if self.seen[e].get(key, 0) >= val:
                continue
            self.seen[e][key] = val
            sem = self.sems[key]
            self.q[e].append(lambda eng, sem=sem, val=val: eng.wait_ge(sem, val))
            self.log[e].append(("wait", key, val))

    @staticmethod
    def _merge(dst, src):
        for k, v in src.items():
            if dst.get(k, 0) < v:
                dst[k] = v

    def _deps(self, e, reads, writes, pwrites):
        deps = {}
        for t in reads:
            self._merge(deps, t.w)
            if t.excl:
                for k, v in t.r.items():
                    if k != e and deps.get(k, 0) < v:
                        deps[k] = v
        for t in writes:
            for k, v in list(t.w.items()) + list(t.r.items()):
                if k != e and deps.get(k, 0) < v:
                    deps[k] = v
        for t in pwrites:
            for k, v in t.r.items():
                if k != e and deps.get(k, 0) < v:
                    deps[k] = v
        return deps

    def op(self, e, fn, reads=(), writes=(), pwrites=(), inc=True):
        self._waits(e, self._deps(e, reads, writes, pwrites))
        if inc:
            self.cnt[e] += 1
            sem = self.sems[e]
            self.q[e].append(lambda eng, fn=fn, sem=sem: fn(eng).then_inc(sem, 1))
            tok = self.cnt[e]
            self.log[e].append(("inc", e, 1))
        else:
            self.q[e].append(lambda eng, fn=fn: fn(eng))
            tok = self.cnt[e] + 1
        for t in reads:
            if t.r.get(e, 0) < tok:
                t.r[e] = tok
        for t in writes:
            t.w = {e: tok}
        for t in pwrites:
            if t.w.get(e, 0) < tok:
                t.w[e] = tok

    def dma(self, e, out_ap, in_ap, reads=(), writes=(), pwrites=(), semt=None, noncontig=False):
        self._waits(e, self._deps(e, reads, writes, pwrites))
        st = semt if semt is not None else (list(writes) + list(pwrites) + list(reads))[0]
        key = self.dma_sem(st)
        st.dcnt += 16
        sem = self.sems[key]
        nc = self.nc
        if noncontig:
            def f(eng, o=out_ap, i=in_ap, sem=sem):
                with nc.allow_non_contiguous_dma(reason="small layout shuffle"):
                    eng.dma_start(out=o, in_=i).then_inc(sem, 16)
        else:
            def f(eng, o=out_ap, i=in_ap, sem=sem):
                eng.dma_start(out=o, in_=i).then_inc(sem, 16)
        self.q[e].append(f)
        self.log[e].append(("inc", key, 16))
        tok = st.dcnt
        for t in reads:
            if t.r.get(key, 0) < tok:
                t.r[key] = tok
        for t in writes:
            t.w = {key: tok}
        for t in pwrites:
            if t.w.get(key, 0) < tok:
                t.w[key] = tok

    def handoff(self, new, olds):
        for o in olds:
            self._merge(new.r, o.r)
            self._merge(new.r, o.w)

    def final_wait(self, e, ts):
        deps = {}
        for t in ts:
            self._merge(deps, t.w)
            self._merge(deps, t.r)
            if t.dkey is not None:
                self._merge(deps, {t.dkey: t.dcnt})
        self._waits(e, deps)


def build_program():
    nc = bass.Bass("TRN2", target_bir_lowering=False)
    es = ExitStack()
    P = Prog(nc, es)

    def din(name, shape):
        return nc.dram_tensor(name, list(shape), F32, kind="ExternalInput").ap()

    x_d = din("x", [S, D])
    ccol_d = din("c_col", [128, 16])
    wada_d = din("w_ada", [D, 6 * D])
    bada_d = din("b_ada", [6 * D])
    badaT_d = din("b_adaT", [128, 96])
    g1T_d = din("g1T", [128, 16])
    g2T_d = din("g2T", [128, 16])
    gfb_d = din("gfinal_b", [128, D])
    win_d = din("w_in", [D, 11776])
    bgT_d = din("b_gateT", [128, 32])
    wps_d = din("w_proj_sb", [1024, D])
    wpd_d = din("w_proj_dil", [512, D])
    wout_d = din("w_out", [D, D])
    wg_d = din("w_ffn_gate", [D, DFF])
    wu_d = din("w_ffn_up", [D, DFF])
    wd_d = din("w_ffn_down", [DFF, D])
    ident_d = din("ident", [128, 128])
    negtri_d = din("negtri", [128, 128])
    negrest_d = din("negrest", [128, 128])
    dmask_d = din("dmask", [128, 128])
    vones_d = din("vones", [128, 3, 128])
    vcol_d = din("vcol", [128, 3])
    biasT_d = din("biasT", [128, 12, 256])
    out_d = nc.dram_tensor("out", [OWN, D], F32, kind="ExternalOutput").ap()
    scr_d = nc.dram_tensor("scr_mod", [6 * D], F32).ap()
    if DEBUG:
        xnew_d = nc.dram_tensor("xnew", [OWN, D], F32, kind="ExternalOutput").ap()
        dbg_hT = nc.dram_tensor("dbg_hT", [128, NCH, S], F32, kind="ExternalOutput").ap()
        dbg_osb = nc.dram_tensor("dbg_osb", [128, 8, OWN], F32, kind="ExternalOutput").ap()
        dbg_odl = nc.dram_tensor("dbg_odl", [128, 4, OWN], F32, kind="ExternalOutput").ap()
        dbg_mT = nc.dram_tensor("dbg_mT", [128, NCH, OWN], F32, kind="ExternalOutput").ap()
        dbg_h2T = nc.dram_tensor("dbg_h2T", [128, NCH, OWN], F32, kind="ExternalOutput").ap()
        dbg_mod = nc.dram_tensor("dbg_mod", [128, 96], F32, kind="ExternalOutput").ap()
    else:
        xnew_d = nc.dram_tensor("xnew", [OWN, D], F32).ap()

    def sb(name, shape, dt):
        return es.enter_context(nc.sbuf_tensor("sb_" + name, list(shape), dt))

    ident = sb("ident", [128, 128], BF16)
    negtri = sb("negtri", [128, 128], BF16)
    negrest = sb("negrest", [128, 128], BF16)
    dmask = sb("dmask", [128, 128], F32)
    vones = sb("vones", [128, 3, 128], BF16)
    vcol = sb("vcol", [128, 3], F32)
    ccol = sb("ccol", [128, 16], F32)
    sc = sb("sc", [128, 16], BF16)
    badaT = sb("badaT", [128, 96], F32)
    modT = sb("modT", [128, 96], F32)
    g1T = sb("g1T", [128, 16], F32)
    g2T = sb("g2T", [128, 16], F32)
    bgT = sb("bgT", [128, 32], F32)
    a1 = sb("a1", [128, 16], F32)
    a2 = sb("a2", [128, 16], F32)
    ss = sb("ss", [128, 8], F32)
    rstd = sb("rstd", [128, 8], F32)
    rowt = sb("rowt", [1, 2, 256], F32)
    browt = sb("browt", [1, 2, 256], F32)
    biasT = sb("biasT", [128, 2, 256], F32)
    onec = sb("onec", [128, 1], F32)
    M = sb("M", [128, 61440], BF16)
    hTb_ap = M[:, 0:16384].rearrange("p (c t) -> p c t", c=16)
    hTa_ap = M[:, 16384:32768].rearrange("p (c t) -> p c t", c=16)
    r1 = M[:, 32768:49152]
    Mf = M[:].bitcast(F32)
    odl = M[:, 49152:53248].rearrange("p (h t) -> p h t", h=4)
    osb = M[:, 53248:61440].rearrange("p (h t) -> p h t", h=8)
    v2b_sb = sb("v2b", [128, 4096], BF16)
    acc = sb("acc", [128, 4096], F32)
    work = sb("work", [128, 3584], F32)
    NRING = 5
    ring = sb("ring", [128, NRING, 4096], BF16)
    ps = [es.enter_context(nc.psum_tensor("ps%d" % i, [128, 512], F32)) for i in range(8)]

    Tc = T(name="consts")
    bank = [T(ps[i], "bank%d" % i) for i in range(8)]
    for t_ in bank:
        t_.excl = True
    ringT = [T(ring[:, i, :], "ring%d" % i) for i in range(NRING)]
    ring_i = [0]

    def ring_next():
        t = ringT[ring_i[0] % NRING]
        ring_i[0] += 1
        return t

    hTt = [T(name="hT%d" % i) for i in range(4)]

    def hsl(k, t0, t1):
        if t0 >= OWN:
            return hTb_ap[:, k, t0 - OWN:t1 - OWN]
        return hTa_ap[:, k, t0:t1]

    xt_t = [T(Mf[:, 16384 + 2048 * i:16384 + 2048 * (i + 1)], "xt%d" % i) for i in range(2)]
    xn_t = [T(r1[:, 8192 + 4096 * i: 8192 + 4096 * (i + 1)], "xn%d" % i) for i in range(2)]
    Q2 = T(r1[:, 0:2048].rearrange("p (j t) -> p j t", j=2), "Q2")
    K2 = T(r1[:, 2048:6144].rearrange("p (j t) -> p j t", j=2), "K2")
    V2 = T(r1[:, 6144:10240].rearrange("p (b c) -> p b c", b=16), "V2")
    Q2b = T(r1[:, 10240:12288].rearrange("p (j t) -> p j t", j=2), "Q2b")
    K2b = T(r1[:, 12288:16384].rearrange("p (j t) -> p j t", j=2), "K2b")
    V2b = T(v2b_sb[:].rearrange("p (b c) -> p b c", b=16), "V2b")
    QKV = [(Q2, K2, V2), (Q2b, K2b, V2b)]
    osbT = T(osb, "osb")
    odlT = T(odl, "odl")
    accn = T(acc[:, 0:2048].rearrange("p (j t) -> p j t", j=2), "accn")
    accd = T(acc[:, 2048:4096].rearrange("p (j t) -> p j t", j=2), "accd")
    g1b = T(acc[:, 0:2048], "g1b")
    g2b = T(acc[:, 2048:4096], "g2b")
    gfb = T(acc[:, 0:2048], "gfb")
    eT = [T(work[:, 512 * i:512 * (i + 1)], "e%d" % i) for i in range(3)]
    ecT = [T(work[:, 1536 + 512 * i:1536 + 512 * (i + 1)], "ec%d" % i) for i in range(2)]
    wbf = work[:].bitcast(BF16)[:, 5120:7168]
    spT = [T(wbf[:, 512 * i:512 * (i + 1)], "sp%d" % i) for i in range(2)]
    aT = [T(wbf[:, 1024 + 512 * i:1024 + 512 * (i + 1)], "a%d" % i) for i in range(2)]
    mT = T(hTa_ap, "mT")
    actT = T(M[:, 16384:61440].rearrange("p (c t) -> p c t", c=NFF), "actT")
    h2Tt = [T(name="h2T%d" % i) for i in range(2)]
    h2T = hTb_ap
    zxt = [T(Mf[:, 2048 * i:2048 * (i + 1)], "zxt%d" % i) for i in range(2)]
    zov = [T(Mf[:, 4096 + 2048 * i:4096 + 2048 * (i + 1)], "zov%d" % i) for i in range(2)]
    modTt = T(modT, "modT")
    a1t, a2t = T(a1, "a1"), T(a2, "a2")
    sst, rstdt = T(ss, "ss"), T(rstd, "rstd")
    sct = T(sc, "sc")
    rowT = [T(rowt[:, i, :], "row%d" % i) for i in range(2)]
    browT = [T(browt[:, i, :], "brow%d" % i) for i in range(2)]
    biasTt = T(biasT, "biasT")
    scrT = T(name="scr")
    xnewT = [T(name="xnew%d" % i) for i in range(8)]
    outT = [T(name="out%d" % i) for i in range(8)]

    DBGT = []

    def dbgT(name):
        t = T(name=name)
        DBGT.append(t)
        return t

    def mm(out_ap, lhsT, rhs, start, stop, reads, bankt, inc):
        P.op("pe", lambda e: e.matmul(out_ap, lhsT=lhsT, rhs=rhs, start=start, stop=stop, skip_group_check=True),
             reads=reads, pwrites=[bankt], inc=inc)

    def evac(out_ap, in_ap, reads, writes=(), pwrites=(), scale=None, bias=None, eng=None):
        if eng is None:
            eng = ("act", "dve")[P.rr % 2]
            P.rr += 1
        if eng == "act":
            kw = {}
            if scale is not None:
                kw["scale"] = scale
            if bias is not None:
                kw["bias"] = bias
            P.op("act", lambda e: e.activation(out=out_ap, in_=in_ap, func=AF.Identity, **kw),
                 reads=reads, writes=writes, pwrites=pwrites)
        else:
            if scale is None and bias is None:
                P.op("dve", lambda e: e.tensor_copy(out=out_ap, in_=in_ap), reads=reads, writes=writes, pwrites=pwrites)
            elif bias is None:
                P.op("dve", lambda e: e.tensor_scalar(out=out_ap, in0=in_ap, scalar1=scale, scalar2=None, op0=ALU.mult),
                     reads=reads, writes=writes, pwrites=pwrites)
            else:
                s1 = scale if scale is not None else 1.0
                P.op("dve", lambda e: e.tensor_scalar(out=out_ap, in0=in_ap, scalar1=s1, scalar2=bias,
                                                      op0=ALU.mult, op1=ALU.add),
                     reads=reads, writes=writes, pwrites=pwrites)

    gb = [0]
    GEN_BANKS = [2, 3]

    def next_bank(pool=None):
        pool = pool or GEN_BANKS
        b = pool[gb[0] % len(pool)]
        gb[0] += 1
        return b

    def wpanel(src_ap, nk):
        t = ring_next()
        dst = t.ap[:, 0:nk * 256].rearrange("p (k n) -> p k n", k=nk)
        P.dma("pool", dst, src_ap.rearrange("(k p) n -> p k n", p=128), writes=[t])
        return t, dst

    def stop(n):
        if STOP == n:
            raise StopBuild()

    try:
        def cload(q, dst, src):
            P.dma(q, dst, src, pwrites=[Tc], semt=Tc)

        cload("pool", ident[:], ident_d)
        cload("pool", negtri[:], negtri_d)
        cload("pool", negrest[:], negrest_d)
        cload("pool", vones[:], vones_d)
        cload("sp", dmask[:], dmask_d)
        cload("sp", vcol[:], vcol_d)
        cload("sp", ccol[:], ccol_d)
        cload("sp", badaT[:], badaT_d)
        cload("sp", g1T[:], g1T_d)
        cload("sp", g2T[:], g2T_d)
        cload("sp", bgT[:], bgT_d)

        P.op("act", lambda e: e.activation(out=sc[:], in_=ccol[:], func=AF.Silu), reads=[Tc], writes=[sct])
        onect = T(onec, "onec")
        P.op("dve", lambda e: e.memset(onec[:], 1.0), writes=[onect])

        PM = 3
        mod_done = [0]

        def mod_panel(pi):
            t, w = wpanel(wada_d[:, pi * 256:(pi + 1) * 256], 16)
            kind = (pi * 256) // D
            if kind in (2, 5):
                b = next_bank()
                for k in range(16):
                    mm(ps[b][0:1, 0:256], sc[:, k:k + 1], w[:, k, :], k == 0, k == 15, [t, sct], bank[b], k == 15)
                i = mod_done[0] % 2
                mod_done[0] += 1
                P.dma("sp", browt[:, i, :], bada_d[pi * 256:(pi + 1) * 256].rearrange("(o n) -> o n", o=1), writes=[browT[i]])
                P.op("dve", lambda e: e.tensor_tensor(out=rowt[:, i, :], in0=ps[b][0:1, 0:256], in1=browt[:, i, :], op=ALU.add),
                     reads=[bank[b], browT[i]], writes=[rowT[i]])
                P.dma("sp", scr_d[pi * 256:(pi + 1) * 256].rearrange("(o n) -> o n", o=1), rowt[:, i, :],
                      reads=[rowT[i]], pwrites=[scrT], semt=rowT[i])
            else:
                b = next_bank()
                for j in range(2):
                    for k in range(16):
                        mm(ps[b][:, j:j + 1], w[:, k, j * 128:(j + 1) * 128], sc[:, k:k + 1],
                           k == 0, k == 15, [t, sct], bank[b], k == 15)
                nch = pi * 2
                P.op("dve", lambda e: e.tensor_tensor(out=modT[:, nch:nch + 2], in0=ps[b][:, 0:2], in1=badaT[:, nch:nch + 2],
                                                      op=ALU.add),
                     reads=[bank[b], Tc], pwrites=[modTt])

        stop(-1)
        TPB = [4, 5, 6, 7]
        tpi = [0]

        def norm_T(pair_i, xts, dstT_ap, dst_t, at, a_ap, b_ap, b_t, col0, xn=None):
            xn = xn if xn is not None else xn_t[pair_i % 2]
            xnv = xn.ap.rearrange("p (u n) -> p u n", u=2)
            for u in range(2):
                xt = xts[u]
                si = (pair_i % 4) * 2 + u
                P.op("act", lambda e, xt=xt, u=u, si=si: e.activation(out=xnv[:, u, :], in_=xt.ap, func=AF.Square,
                                                                      accum_out=ss[:, si:si + 1]),
                     reads=[xt], pwrites=[xn, sst])
                P.op("dve", lambda e, si=si: e.tensor_scalar(out=rstd[:, si:si + 1], in0=ss[:, si:si + 1], scalar1=1.0 / D,
                                                             scalar2=EPS, op0=ALU.mult, op1=ALU.add),
                     reads=[sst], pwrites=[rstdt])
                P.op("act", lambda e, si=si: e.activation(out=rstd[:, si:si + 1], in_=rstd[:, si:si + 1], func=AF.Sqrt),
                     reads=[rstdt], pwrites=[rstdt])
                P.op("dve", lambda e, si=si: e.reciprocal(out=rstd[:, si:si + 1], in_=rstd[:, si:si + 1]),
                     reads=[rstdt], pwrites=[rstdt])
                P.op("dve", lambda e, xt=xt, u=u, si=si: e.tensor_scalar(out=xnv[:, u, :], in0=xt.ap,
                                                                        scalar1=rstd[:, si:si + 1], scalar2=None, op0=ALU.mult),
                     reads=[xt, rstdt], pwrites=[xn])
            for c4 in range(4):
                b = TPB[tpi[0] % len(TPB)]
                tpi[0] += 1
                pv = ps[b][:].bitcast(BF16)
                for cc in range(4):
                    c = c4 * 4 + cc
                    for u in range(2):
                        last = (cc == 3 and u == 1)
                        o = pv[:, cc * 256 + u * 128: cc * 256 + (u + 1) * 128]
                        i_ = xnv[:, u, c * 128:(c + 1) * 128]
                        P.op("pe", lambda e, o=o, i_=i_: e.transpose(o, i_, ident[:]),
                             reads=[xn, Tc], pwrites=[bank[b]], inc=last)
                for cc in range(4):
                    c = c4 * 4 + cc
                    if at is None:
                        evac(dstT_ap[:, c, col0:col0 + 256], pv[:, cc * 256:(cc + 1) * 256], reads=[bank[b]],
                             pwrites=[dst_t], eng=("act", "dve")[c4 % 2])
                    else:
                        evac(dstT_ap[:, c, col0:col0 + 256], pv[:, cc * 256:(cc + 1) * 256], reads=[bank[b], at, b_t],
                             pwrites=[dst_t], scale=a_ap[:, c:c + 1], bias=b_ap[:, c:c + 1], eng=("act", "dve")[c4 % 2])

        hTraw = [T(name="hTraw%d" % i) for i in range(4)]
        for pr in range(8):
            mod_panel(2 * pr)
            mod_panel(2 * pr + 1)
            xts = []
            for u in range(2):
                xt = xt_t[u]
                tb = pr * 2 + u
                P.dma("sp", xt.ap, x_d[tb * 128:(tb + 1) * 128, :], writes=[xt])
                xts.append(xt)
            tg = pr // 2
            norm_T(pr, xts, hTa_ap if pr < 4 else hTb_ap, hTraw[tg], None, None, None, None, (pr % 4) * 256)
        P.op("dve", lambda e: e.scalar_tensor_tensor(out=a1[:], in0=modT[:, 16:32], scalar=1.0, in1=g1T[:],
                                                     op0=ALU.add, op1=ALU.mult),
             reads=[modTt, Tc], writes=[a1t])
        if DEBUG:
            P.dma("sp", dbg_mod, modT[:], reads=[modTt], semt=dbgT("dbg0a"))
        stop(0)
        for tg in (2, 3, 0, 1):
            eng = ("act", "dve")[tg % 2]
            for c in range(16):
                ap_ = hsl(c, tg * 512, (tg + 1) * 512)
                if eng == "act":
                    P.op("act", lambda e, ap_=ap_, c=c: e.activation(out=ap_, in_=ap_, func=AF.Identity,
                                                                    scale=a1[:, c:c + 1], bias=modT[:, c:c + 1]),
                         reads=[hTraw[tg], a1t, modTt], pwrites=[hTt[tg]])
                else:
                    P.op("dve", lambda e, ap_=ap_, c=c: e.tensor_scalar(out=ap_, in0=ap_, scalar1=a1[:, c:c + 1],
                                                                       scalar2=modT[:, c:c + 1], op0=ALU.mult, op1=ALU.add),
                         reads=[hTraw[tg], a1t, modTt], pwrites=[hTt[tg]])

        if DEBUG:
            P.dma("pool", dbg_hT[:, :, 0:OWN], hTa_ap, reads=hTt, semt=dbgT("dbg1"))
            P.dma("pool", dbg_hT[:, :, OWN:S], hTb_ap, reads=hTt, semt=dbgT("dbg1b"))
            P.dma("sp", dbg_mod, modT[:], reads=[modTt], semt=dbgT("dbg0"))

        stop(1)
        for t in (Q2, K2, V2, Q2b, K2b):
            P.handoff(t, xt_t + xn_t)

        def project_thunks(qc0, kc0, vc0, d, qkv, ev_eng):
            Q2_, K2_, V2_ = qkv
            st = {}
            th = []
            pend = []

            def flush(keep):
                while len(pend) > keep:
                    pend.pop(0)()

            def load():
                st["q"] = wpanel(win_d[:, qc0:qc0 + 256], 16)
                st["k"] = wpanel(win_d[:, kc0:kc0 + 256], 16)
                st["v"] = wpanel(win_d[:, vc0:vc0 + 256], 16)
            th.append(load)

            def fm_group(which, dstT, j, tg, t0, scale, part, hold):
                tq, wq = st[which]
                if part == 0:
                    hold["b"] = next_bank()
                b = hold["b"]
                for k in range(part * 4, part * 4 + 4):
                    mm(ps[b][:, :], wq[:, k, j * 128:(j + 1) * 128], hsl(k, t0, t0 + 512),
                       k == 0, k == 15, [tq, hTt[t0 // 512]], bank[b], k == 15)
                if part < 3:
                    return
                if d == 1:
                    dst = dstT.ap[:, j, tg * 512:(tg + 1) * 512]
                    src = ps[b][:, :]
                else:
                    ni = 512 // d
                    dst = dstT.ap[:, j, :].rearrange("p (r i) -> p r i", r=d)[:, :, tg * ni:(tg + 1) * ni]
                    src = ps[b][:, :].rearrange("p (i r) -> p r i", r=d)
                pend.append(lambda: evac(dst, src, reads=[bank[b]], pwrites=[dstT], scale=scale, eng=ev_eng))

            for j in range(2):
                for tg in range(2):
                    hold = {}
                    for part in range(4):
                        th.append(lambda j=j, tg=tg, part=part, hold=hold: fm_group("q", Q2_, j, tg, OWN + tg * 512, QS, part, hold))
            for j in range(2):
                for tg in range(4):
                    hold = {}
                    for part in range(4):
                        th.append(lambda j=j, tg=tg, part=part, hold=hold: fm_group("k", K2_, j, tg, tg * 512, None, part, hold))

            def v_group(tb, part, hold):
                tv, wv = st["v"]
                if d == 1:
                    parts = [(slice(0, 128), lambda k: hsl(k, tb * 128, (tb + 1) * 128))]
                    rd = [hTt[tb // 4]]
                    vk = 0 if tb < 8 else 1
                elif d == 4:
                    r, ib = tb // 4, tb % 4
                    parts = [(slice(0, 128), lambda k: hsl(k, ib * 512, (ib + 1) * 512).rearrange(
                        "p (i r) -> p r i", r=4)[:, r, :])]
                    rd = [hTt[ib]]
                    vk = 0 if ib < 2 else 1
                else:
                    r = tb
                    parts = [(slice(0, 64), lambda k: hTa_ap[:, k, :].rearrange("p (i r) -> p r i", r=16)[:, r, :]),
                             (slice(64, 128), lambda k: hTb_ap[:, k, :].rearrange("p (i r) -> p r i", r=16)[:, r, :])]
                    rd = hTt
                    vk = 2
                if part == 0:
                    hold["b"] = next_bank()
                b = hold["b"]
                allmm = [(pi_, k) for pi_ in range(len(parts)) for k in range(16)]
                q4 = len(allmm) // 4
                for (pi_, k) in allmm[part * q4:(part + 1) * q4]:
                    psl, tsel = parts[pi_]
                    mm(ps[b][psl, 0:256], tsel(k), wv[:, k, :], k == 0, k == 15, [tv] + list(rd), bank[b],
                       k == 15 and pi_ == len(parts) - 1)
                if part == 3:
                    pend.append(lambda: evac(V2_.ap[:, tb, :], ps[b][:, 0:256], reads=[bank[b], Tc], pwrites=[V2_],
                                             scale=vcol[:, vk:vk + 1], eng=ev_eng))

            for tb in range(16):
                hold = {}
                for part in range(4):
                    th.append(lambda tb=tb, part=part, hold=hold: v_group(tb, part, hold))

            def wrap(f, last):
                def g():
                    cnt_before = len(pend)
                    f()
                    if len(pend) > cnt_before:
                        pend[-1].age = 0
                    for p_ in list(pend):
                        p_.age = getattr(p_, 'age', 0) + 1
                    while pend and (pend[0].age > 2 or last):
                        pend.pop(0)()
                return g
            return [wrap(f, i == len(th) - 1) for i, f in enumerate(th)]

        def sb_thunks(hp, qkv):
            Q2_, K2_, V2_ = qkv
            ZB = [0, 1]
            seq = []
            for j in range(2):
                streams = []
                for qc in range(2):
                    qb0 = 8 + 4 * qc
                    tiles = []
                    for kb in range(qb0 + 3, -1, -1):
                        c0 = max(0, kb - qb0) * 128
                        tiles.append((j, hp * 2 + j, qc, kb, c0, kb >= qb0))
                    streams.append(tiles)
                for i in range(max(len(s_) for s_ in streams)):
                    for s_ in streams:
                        if i < len(s_):
                            seq.append(s_[i])
            Pb = {0: 4, 1: 6}
            Ob = {0: 5, 1: 7}
            first = {(0, 0): True, (0, 1): True, (1, 0): True, (1, 1): True}
            st = {}

            n = len(seq)

            def zmm(i):
                j, head, qc, kb, c0, diag = seq[i]
                N = 512 - c0
                zb = ZB[i % 2]
                mm(ps[zb][:, 0:N], K2_.ap[:, j, kb * 128:(kb + 1) * 128], Q2_.ap[:, j, qc * 512 + c0:(qc + 1) * 512],
                   True, True, [K2_, Q2_], bank[zb], True)

            def expz(i):
                j, head, qc, kb, c0, diag = seq[i]
                N = 512 - c0
                zb = ZB[i % 2]
                e_ = eT[i % 3]
                P.op("act", lambda e: e.activation(out=e_.ap[:, 0:N], in_=ps[zb][:, 0:N], func=AF.Exp),
                     reads=[bank[zb]], writes=[e_])
                if diag:
                    P.op("dve", lambda e: e.tensor_tensor(out=e_.ap[:, 0:128], in0=e_.ap[:, 0:128], in1=dmask[:], op=ALU.mult),
                         reads=[e_, Tc], pwrites=[e_])

            def ln(i):
                j, head, qc, kb, c0, diag = seq[i]
                N = 512 - c0
                e_, sp_ = eT[i % 3], spT[i % 2]
                vk = 0 if kb < 8 else 1
                P.op("act", lambda e: e.activation(out=sp_.ap[:, 0:N], in_=e_.ap[:, 0:N], func=AF.Ln, bias=onec[:, 0:1],
                                                   scale=vcol[:, vk:vk + 1]),
                     reads=[e_, Tc, onect], writes=[sp_])

            def tri(i):
                j, head, qc, kb, c0, diag = seq[i]
                N = 512 - c0
                sp_ = spT[i % 2]
                fst = first[(j, qc)]
                first[(j, qc)] = False
                st[i] = fst
                mm(ps[Pb[qc]][:, c0:512], negtri[:], sp_.ap[:, 0:N], fst, True, [sp_, Tc], bank[Pb[qc]], True)

            def expc(i):
                j, head, qc, kb, c0, diag = seq[i]
                N = 512 - c0
                ec_ = ecT[i % 2]
                P.op("act", lambda e: e.activation(out=ec_.ap[:, 0:N], in_=ps[Pb[qc]][:, c0:512], func=AF.Exp),
                     reads=[bank[Pb[qc]]], writes=[ec_])

            def rest(i):
                j, head, qc, kb, c0, diag = seq[i]
                N = 512 - c0
                sp_ = spT[i % 2]
                if kb > 0:
                    mm(ps[Pb[qc]][:, c0:512], negrest[:], sp_.ap[:, 0:N], False, True, [sp_, Tc], bank[Pb[qc]], True)

            def av(i):
                j, head, qc, kb, c0, diag = seq[i]
                N = 512 - c0
                e_, ec_, a_ = eT[i % 3], ecT[i % 2], aT[i % 2]
                ob = Ob[qc]
                P.op("dve", lambda e: e.tensor_tensor(out=a_.ap[:, 0:N], in0=e_.ap[:, 0:N], in1=ec_.ap[:, 0:N], op=ALU.mult),
                     reads=[e_, ec_], writes=[a_])
                mm(ps[ob][:, c0:512], V2_.ap[:, kb, j * 128:(j + 1) * 128], a_.ap[:, 0:N], st[i], True, [V2_, a_], bank[ob], True)
                if kb == 0:
                    evac(osb[:, head, qc * 512:(qc + 1) * 512], ps[ob][:, :], reads=[bank[ob]], pwrites=[osbT])

            def it(i):
                if i == -2:
                    zmm(0)
                if 0 <= i + 2 < n:
                    expz(i + 2)
                if 0 <= i + 1 < n:
                    ln(i + 1)
                if i >= 0:
                    expc(i)
                    rest(i)
                if 0 <= i + 1 < n:
                    tri(i + 1)
                if 0 <= i + 3 < n:
                    zmm(i + 3)
                if i >= 0:
                    av(i)

            th = [lambda i=i: it(i) for i in range(-2, n)]
            return th

        def dil_thunks(g, d, first_group, qkv):
            Q2_, K2_, V2_ = qkv
            L = S // d
            nb = max(1, L // 128)
            DB = [0, 1, 4, 5, 6, 7]
            cnt = [0]
            th = []

            units = []
            for j in range(2):
                for r in range(d):
                    blocks = [None] if d == 16 else list(range(nb // 2, nb))
                    for qb in blocks:
                        units.append((j, r, qb))
            info = {}

            def s1(u):
                j, r, qb = units[u]
                zb = DB[u % 3]
                w = u % 2
                if d == 16:
                    N = 64
                    mm(ps[zb][:, 0:64], K2_.ap[:, j, r * 128:(r + 1) * 128], Q2_.ap[:, j, r * 64:(r + 1) * 64],
                       True, True, [K2_, Q2_], bank[zb], True)
                    bsl = biasT[:, j, 192:256]
                    kinds = [2]
                    vblocks = [r]
                    accv = lambda A: A.ap[:, j, :].rearrange("p (i r) -> p r i", r=16)[:, r, 0:64]
                    NQ = 64
                else:
                    N = 256
                    nbl = L // 128
                    kprev = r * nbl + qb - 1
                    kcur = r * nbl + qb
                    qoff = r * (L // 2) + (qb - nb // 2) * 128
                    mm(ps[zb][:, 0:128], K2_.ap[:, j, kprev * 128:(kprev + 1) * 128], Q2_.ap[:, j, qoff:qoff + 128],
                       True, True, [K2_, Q2_], bank[zb], False)
                    mm(ps[zb][:, 128:256], K2_.ap[:, j, kcur * 128:(kcur + 1) * 128], Q2_.ap[:, j, qoff:qoff + 128],
                       True, True, [K2_, Q2_], bank[zb], True)
                    bsl = biasT[:, j, 0:256]
                    half = nb // 2
                    kinds = [0 if (qb - 1) < half else 1, 0 if qb < half else 1]
                    vblocks = [kprev, kcur]
                    i0 = (qb - nb // 2) * 128
                    if d == 1:
                        accv = lambda A: A.ap[:, j, i0:i0 + 128]
                    else:
                        accv = lambda A: A.ap[:, j, :].rearrange("p (i r) -> p r i", r=d)[:, r, i0:i0 + 128]
                    NQ = 128
                s_, p_ = eT[w], aT[w]
                P.op("dve", lambda e: e.tensor_tensor(out=s_.ap[:, 0:N], in0=ps[zb][:, 0:N], in1=bsl, op=ALU.add),
                     reads=[bank[zb], biasTt], writes=[s_])
                P.op("act", lambda e: e.activation(out=p_.ap[:, 0:N], in_=s_.ap[:, 0:N], func=AF.Exp),
                     reads=[s_], writes=[p_])
                info[u] = (p_, kinds, vblocks, accv, NQ, j)

            def s2(u):
                p_, kinds, vblocks, accv, NQ, j = info[u]
                nbk = DB[3 + u % 3]
                nk = len(kinds)
                for i in range(nk):
                    mm(ps[nbk][:, 0:NQ], V2_.ap[:, vblocks[i], j * 128:(j + 1) * 128], p_.ap[:, i * NQ:(i + 1) * NQ],
                       i == 0, i == nk - 1, [V2_, p_], bank[nbk], False)
                for i in range(nk):
                    mm(ps[nbk][:, 128:128 + NQ], vones[:, kinds[i], :], p_.ap[:, i * NQ:(i + 1) * NQ],
                       i == 0, i == nk - 1, [Tc, p_], bank[nbk], i == nk - 1)
                for (A, c0) in ((accn, 0), (accd, 128)):
                    av_ = accv(A)
                    if first_group:
                        P.op("dve", lambda e, av_=av_, c0=c0: e.tensor_copy(out=av_, in_=ps[nbk][:, c0:c0 + NQ]),
                             reads=[bank[nbk]], pwrites=[A])
                    else:
                        P.op("dve", lambda e, av_=av_, c0=c0: e.tensor_tensor(out=av_, in0=av_, in1=ps[nbk][:, c0:c0 + NQ], op=ALU.add),
                             reads=[bank[nbk], A], pwrites=[A])

            nu = len(units)

            def it(u):
                if u + 1 < nu:
                    s1(u + 1)
                if u >= 0:
                    s2(u)
            th = [lambda u=u: it(u) for u in range(-1, nu)]
            return th

        mod_next = [16]

        def mod_thunks(n):
            th = []
            for _ in range(n):
                if mod_next[0] < 48:
                    th.append(lambda pi=mod_next[0]: mod_panel(pi))
                    mod_next[0] += 1
            return th

        def finish_pair(pair):
            for j in range(2):
                hs = pair * 2 + j
                P.op("dve", lambda e, j=j: e.reciprocal(out=accd.ap[:, j, :], in_=accd.ap[:, j, :]),
                     reads=[accd], pwrites=[accd])
                P.op("dve", lambda e, j=j, hs=hs: e.tensor_tensor(out=odl[:, hs, :], in0=accn.ap[:, j, :], in1=accd.ap[:, j, :],
                                                                 op=ALU.mult),
                     reads=[accn, accd], pwrites=[odlT])

        DIL = ((128, 1), (512, 4), (2048, 16))
        rounds = []
        for hp in range(4):
            rounds.append(("sb", hp, (hp * 256, 1024 + hp * 256, 2048 + hp * 256, 1)))
        for pair in range(2):
            for g, (win, d) in enumerate(DIL):
                off = g * 512 + pair * 256
                rounds.append(("dil", (pair, g, d), (3072 + off, 4608 + off, 6144 + off, d)))

        def att_thunks(ri):
            kind, info, _ = rounds[ri]
            qkv = QKV[ri % 2]
            th = []
            if kind == "sb":
                th += sb_thunks(info, qkv)
            else:
                pair, g, d = info
                th.append(lambda: P.dma("sp", biasT[:], biasT_d[:, g * 4 + pair * 2: g * 4 + pair * 2 + 2, :], writes=[biasTt]))
                th += dil_thunks(g, d, g == 0, qkv)
                if g == 2:
                    th.append(lambda: finish_pair(pair))
            return th

        def proj_thunks(ri):
            _, _, (qc0, kc0, vc0, d) = rounds[ri]
            prev_kind = rounds[ri - 1][0] if ri > 0 else 'dil'
            return project_thunks(qc0, kc0, vc0, d, QKV[ri % 2], 'dve' if prev_kind == 'sb' else 'act') + mod_thunks(4)

        def interleave(A, B):
            na, nb_ = len(A), len(B)
            bi = 0
            for k, a_ in enumerate(A):
                a_()
                while bi < nb_ and (bi + 1) * na <= (k + 1) * nb_:
                    B[bi]()
                    bi += 1
            while bi < nb_:
                B[bi]()
                bi += 1

        for th_ in proj_thunks(0):
            th_()
        stop(2)
        for ri in range(len(rounds)):
            nxt = proj_thunks(ri + 1) if ri + 1 < len(rounds) else mod_thunks(48)
            interleave(att_thunks(ri), nxt)
            if ri == 0:
                if DEBUG:
                    P.dma("pool", dbg_osb, osb, reads=[osbT], semt=dbgT("dbg2a"))
                stop(3)
        for th_ in mod_thunks(48):
            th_()
        if DEBUG:
            P.dma("pool", dbg_osb, osb, reads=[osbT], semt=dbgT("dbg2"))
            P.dma("pool", dbg_odl, odl, reads=[odlT], semt=dbgT("dbg3"))

        stop(5)
        P.op("dve", lambda e: e.scalar_tensor_tensor(out=a2[:], in0=modT[:, 64:80], scalar=1.0, in1=g2T[:],
                                                     op0=ALU.add, op1=ALU.mult),
             reads=[modTt, Tc], writes=[a2t])
        if DEBUG:
            P.dma("sp", dbg_mod, modT[:], reads=[modTt], semt=dbgT("dbg0c"))
        P.handoff(g1b, [accn, accd])
        P.handoff(g2b, [accn, accd])
        P.dma("sp", g1b.ap, scr_d[2 * D:3 * D].partition_broadcast(128), reads=[scrT], writes=[g1b])
        P.dma("sp", g2b.ap, scr_d[5 * D:6 * D].partition_broadcast(128), reads=[scrT], writes=[g2b])

        P.handoff(mT, hTt[0:2])
        for c2 in range(8):
            tgs, wgs = wpanel(win_d[:, 7680 + c2 * 256: 7680 + (c2 + 1) * 256], 16)
            tgd, wgd = wpanel(win_d[:, 9728 + c2 * 256: 9728 + (c2 + 1) * 256], 16)
            tp = ring_next()
            wps = tp.ap[:, 0:2048].rearrange("p (k n) -> p k n", k=8)
            wpd = tp.ap[:, 2048:3072].rearrange("p (k n) -> p k n", k=4)
            P.dma("pool", wps, wps_d[:, c2 * 256:(c2 + 1) * 256].rearrange("(k p) n -> p k n", p=128), writes=[tp])
            P.dma("pool", wpd, wpd_d[:, c2 * 256:(c2 + 1) * 256].rearrange("(k p) n -> p k n", p=128), pwrites=[tp])
            for j2 in range(2):
                c = c2 * 2 + j2
                cs = slice(j2 * 128, (j2 + 1) * 128)
                for tg in range(2):
                    ts = slice(tg * 512, (tg + 1) * 512)
                    bA, bB, bC, bD = [next_bank([0, 1, 2, 3, 4, 5, 6, 7]) for _ in range(4)]
                    for k in range(8):
                        mm(ps[bA][:, :], wps[:, k, cs], osb[:, k, ts], k == 0, k == 7, [tp, osbT], bank[bA], k == 7)
                    for k in range(4):
                        mm(ps[bB][:, :], wpd[:, k, cs], odl[:, k, ts], k == 0, k == 3, [tp, odlT], bank[bB], k == 3)
                    for k in range(16):
                        mm(ps[bC][:, :], wgs[:, k, cs], hsl(k, OWN + tg * 512, OWN + (tg + 1) * 512), k == 0, k == 15,
                           [tgs, hTt[2 + tg]], bank[bC], k == 15)
                    for k in range(16):
                        mm(ps[bD][:, :], wgd[:, k, cs], hsl(k, OWN + tg * 512, OWN + (tg + 1) * 512), k == 0, k == 15,
                           [tgd, hTt[2 + tg]], bank[bD], k == 15)
                    sg, sd = eT[0], eT[1]
                    t1, t2 = ecT[0], ecT[1]
                    P.op("act", lambda e, c=c, bC=bC, sg=sg: e.activation(out=sg.ap, in_=ps[bC][:, :], func=AF.Sigmoid,
                                                                       bias=bgT[:, c:c + 1]),
                         reads=[bank[bC], Tc], writes=[sg])
                    P.op("act", lambda e, c=c, bD=bD, sd=sd: e.activation(out=sd.ap, in_=ps[bD][:, :], func=AF.Sigmoid,
                                                                       bias=bgT[:, 16 + c:17 + c]),
                         reads=[bank[bD], Tc], writes=[sd])
                    P.op("dve", lambda e, bA=bA, sg=sg, t1=t1: e.tensor_tensor(out=t1.ap, in0=ps[bA][:, :], in1=sg.ap, op=ALU.mult),
                         reads=[bank[bA], sg], writes=[t1])
                    P.op("dve", lambda e, bB=bB, sd=sd, t2=t2: e.tensor_tensor(out=t2.ap, in0=ps[bB][:, :], in1=sd.ap, op=ALU.mult),
                         reads=[bank[bB], sd], writes=[t2])
                    P.op("dve", lambda e, c=c, ts=ts, t1=t1, t2=t2: e.tensor_tensor(out=mT.ap[:, c, ts], in0=t1.ap, in1=t2.ap, op=ALU.add),
                         reads=[t1, t2], pwrites=[mT])
        if DEBUG:
            P.dma("pool", dbg_mT, mT.ap, reads=[mT], semt=dbgT("dbg4"))

        stop(6)
        xt4 = [T(Mf[:, 16384 + 2048 * i:16384 + 2048 * (i + 1)], "xt4_%d" % i) for i in range(4)]
        xn5 = [T(M[:, 53248 + 4096 * i:53248 + 4096 * (i + 1)], "xn5_%d" % i) for i in range(2)]
        for t in xt4:
            P.handoff(t, [Q2, K2, V2, Q2b, K2b] + xt_t + xn_t)
        for t in xn5:
            P.handoff(t, [osbT])
        for t in h2Tt:
            P.handoff(t, hTt[2:4])
        for qd in range(2):
            for u in range(4):
                tb = qd * 4 + u
                P.dma("sp", xt4[u].ap, x_d[OWN + tb * 128: OWN + (tb + 1) * 128, :], writes=[xt4[u]])
            for cg8 in range(8):
                two, wo_ = wpanel(wout_d[:, cg8 * 256:(cg8 + 1) * 256], 16)
                for u in range(4):
                    tb = qd * 4 + u
                    xt = xt4[u]
                    col = cg8 * 256
                    b = next_bank([0, 1, 2, 3])
                    for k in range(16):
                        mm(ps[b][:, 0:256], mT.ap[:, k, tb * 128:(tb + 1) * 128], wo_[:, k, :],
                           k == 0, k == 15, [mT, two], bank[b], k == 15)
                    tmp = ecT[u % 2]
                    P.op("dve", lambda e, b=b, col=col, tmp=tmp: e.tensor_tensor(out=tmp.ap[:, 0:256], in0=ps[b][:, 0:256],
                                                                              in1=g1b.ap[:, col:col + 256], op=ALU.mult),
                         reads=[bank[b], g1b], writes=[tmp])
                    P.op("dve", lambda e, xt=xt, col=col, tmp=tmp: e.tensor_tensor(out=xt.ap[:, col:col + 256],
                                                                                in0=xt.ap[:, col:col + 256], in1=tmp.ap[:, 0:256],
                                                                                op=ALU.add),
                         reads=[tmp, xt], pwrites=[xt])
            for u in range(4):
                tb = qd * 4 + u
                P.dma("sp", xnew_d[tb * 128:(tb + 1) * 128, :], xt4[u].ap, reads=[xt4[u]], writes=[xnewT[tb]], semt=xnewT[tb])
            for hp_ in range(2):
                pr = qd * 2 + hp_
                norm_T(pr, [xt4[2 * hp_], xt4[2 * hp_ + 1]], h2T, h2Tt[pr // 2], a2t, a2, modT[:, 48:64], modTt, pr * 256,
                       xn=xn5[hp_])
        if DEBUG:
            P.dma("pool", dbg_h2T, h2T, reads=h2Tt, semt=dbgT("dbg5"))

        stop(7)
        P.handoff(actT, [mT, Q2, K2, V2, Q2b, K2b, osbT, odlT] + xt_t + xn_t + xt4 + xn5)
        for p in range(NFF // 2):
            tg_, wg_ = wpanel(wg_d[:, p * 256:(p + 1) * 256], 16)
            tu_, wu_ = wpanel(wu_d[:, p * 256:(p + 1) * 256], 16)
            for j2 in range(2):
                c = p * 2 + j2
                cs = slice(j2 * 128, (j2 + 1) * 128)
                for tg in range(2):
                    ts = slice(tg * 512, (tg + 1) * 512)
                    bG, bU = [next_bank([0, 1, 2, 3, 4, 5, 6, 7]) for _ in range(2)]
                    for k in range(16):
                        mm(ps[bG][:, :], wg_[:, k, cs], h2T[:, k, ts], k == 0, k == 15, [tg_, h2Tt[tg]], bank[bG], k == 15)
                    for k in range(16):
                        mm(ps[bU][:, :], wu_[:, k, cs], h2T[:, k, ts], k == 0, k == 15, [tu_, h2Tt[tg]], bank[bU], k == 15)
                    sg = eT[(c * 2 + tg) % 2]
                    P.op("act", lambda e, bG=bG, sg=sg: e.activation(out=sg.ap, in_=ps[bG][:, :], func=AF.Silu),
                         reads=[bank[bG]], writes=[sg])
                    P.op("dve", lambda e, bU=bU, sg=sg, c=c, ts=ts: e.tensor_tensor(out=actT.ap[:, c, ts], in0=ps[bU][:, :], in1=sg.ap,
                                                                                 op=ALU.mult),
                         reads=[bank[bU], sg], pwrites=[actT])
        stop(8)
        xpT = [T(Mf[:, 512 * i:512 * (i + 1)], "xp%d" % i) for i in range(8)]
        ypT = [T(Mf[:, 4096 + 512 * i:4096 + 512 * (i + 1)], "yp%d" % i) for i in range(8)]
        for t in xpT + ypT:
            P.handoff(t, h2Tt)
        ydT = [T(name="yd%d" % i) for i in range(2)]
        for cg in range(4):
            for tb in range(8):
                P.dma("sp", xpT[tb].ap, xnew_d[tb * 128:(tb + 1) * 128, cg * 512:(cg + 1) * 512], reads=[xnewT[tb]],
                      writes=[xpT[tb]], semt=ydT[0])
            for tb in range(8):
                xpT[tb].w = {ydT[0].dkey: ydT[0].dcnt}
            for c8 in range(6):
                nk = 8 if c8 < 5 else 4
                t = ring_next()
                wd_ = t.ap[:, 0:nk * 512].rearrange("p (k n) -> p k n", k=nk)
                P.dma("pool", wd_, wd_d[c8 * 1024: c8 * 1024 + nk * 128, cg * 512:(cg + 1) * 512].rearrange("(k p) n -> p k n", p=128),
                      writes=[t])
                for kk in range(nk):
                    c = c8 * 8 + kk
                    for tb in range(8):
                        mm(ps[tb][:, :], actT.ap[:, c, tb * 128:(tb + 1) * 128], wd_[:, kk, :], c == 0, c == NFF - 1,
                           [actT, t], bank[tb], c == NFF - 1 or (kk == nk - 1 and tb == 7))
            for tb in range(8):
                xp, yp = xpT[tb], ypT[tb]
                P.op("dve", lambda e, tb=tb, cg=cg, yp=yp: e.tensor_tensor(out=yp.ap, in0=ps[tb][:, :],
                                                                        in1=g2b.ap[:, cg * 512:(cg + 1) * 512], op=ALU.mult),
                     reads=[bank[tb], g2b], writes=[yp])
                P.op("dve", lambda e, xp=xp, yp=yp: e.tensor_tensor(out=yp.ap, in0=yp.ap, in1=xp.ap, op=ALU.add),
                     reads=[xp, yp], pwrites=[yp])
                P.dma("sp", xnew_d[tb * 128:(tb + 1) * 128, cg * 512:(cg + 1) * 512], yp.ap, reads=[yp], pwrites=[xnewT[tb]],
                      semt=ydT[1])
            for tb in range(8):
                ypT[tb].r[ydT[1].dkey] = ydT[1].dcnt
                xnewT[tb].w[ydT[1].dkey] = ydT[1].dcnt

        stop(9)
        zx4 = [T(Mf[:, 2048 * i:2048 * (i + 1)], "zx%d" % i) for i in range(4)]
        zo4 = [T(Mf[:, 8192 + 2048 * i:8192 + 2048 * (i + 1)], "zo%d" % i) for i in range(4)]
        for t in zx4:
            P.handoff(t, h2Tt + xpT + ypT)
        for t in zo4:
            P.handoff(t, [actT, mT])
        P.handoff(gfb, [g1b, accn, accd])
        P.dma("sp", gfb.ap, gfb_d, writes=[gfb])
        for tb in range(8):
            xt = zx4[tb % 4]
            ot = zo4[tb % 4]
            P.dma("pool", xt.ap, xnew_d[tb * 128:(tb + 1) * 128, :], reads=[xnewT[tb]], writes=[xt])
            si = tb
            P.op("act", lambda e, xt=xt, ot=ot, si=si: e.activation(out=ot.ap, in_=xt.ap, func=AF.Square, accum_out=ss[:, si:si + 1]),
                 reads=[xt], writes=[ot], pwrites=[sst])
            P.op("dve", lambda e, si=si: e.tensor_scalar(out=rstd[:, si:si + 1], in0=ss[:, si:si + 1], scalar1=1.0 / D, scalar2=EPS,
                                                         op0=ALU.mult, op1=ALU.add), reads=[sst], pwrites=[rstdt])
            P.op("act", lambda e, si=si: e.activation(out=rstd[:, si:si + 1], in_=rstd[:, si:si + 1], func=AF.Sqrt),
                 reads=[rstdt], pwrites=[rstdt])
            P.op("dve", lambda e, si=si: e.reciprocal(out=rstd[:, si:si + 1], in_=rstd[:, si:si + 1]),
                 reads=[rstdt], pwrites=[rstdt])
            P.op("dve", lambda e, xt=xt, ot=ot, si=si: e.scalar_tensor_tensor(out=ot.ap, in0=xt.ap, scalar=rstd[:, si:si + 1],
                                                                              in1=gfb.ap, op0=ALU.mult, op1=ALU.mult),
                 reads=[xt, rstdt, gfb], writes=[ot])
            P.dma("sp", out_d[tb * 128:(tb + 1) * 128, :], ot.ap, reads=[ot], writes=[outT[tb]], semt=outT[tb])
        P.final_wait("sp", outT)
        if DEBUG:
            P.final_wait("pool", DBGT)

    except StopBuild:
        P.final_wait("sp", DBGT)
        P.final_wait("pool", DBGT)

    with nc.Block() as block:
        @block.tensor
        def _(e):
            for f in P.q["pe"]:
                f(e)

        @block.scalar
        def _(e):
            for f in P.q["act"]:
                f(e)

        @block.vector
        def _(e):
            for f in P.q["dve"]:
                f(e)

        @block.gpsimd
        def _(e):
            for f in P.q["pool"]:
                f(e)

        @block.sync
        def _(e):
            for f in P.q["sp"]:
                f(e)
    es.close()
    build_program.prog = P
    return nc


def host_consts(r):
    v = np.float32(1.0 if r == 1 else 0.0)
    i = np.arange(128)
    c = {}
    c["ident"] = np.eye(128, dtype=np.float32)
    c["negtri"] = -(i[:, None] >= i[None, :]).astype(np.float32)
    c["negrest"] = -(i[:, None] < i[None, :]).astype(np.float32)
    c["dmask"] = (i[None, :] > i[:, None]).astype(np.float32)
    vo = np.ones((128, 3, 128), np.float32)
    vo[:, 0, :] = v
    vo[:64, 2, :] = v
    c["vones"] = vo
    vc = np.ones((128, 3), np.float32)
    vc[:, 0] = v
    vc[:64, 2] = v
    c["vcol"] = vc
    slopes = (2.0 ** (-8.0 * (np.arange(12, dtype=np.float32) + 1.0) / 12)).astype(np.float32)
    dil = [1, 4, 16]
    bt = np.full((128, 12, 256), -30000.0, np.float32)
    b_ = i[:, None]
    a_ = i[None, :]
    for g in range(3):
        for hs in range(4):
            sl = slopes[g * 4 + hs] * dil[g]
            rel = 128 + a_ - b_
            bt[:, g * 4 + hs, 0:128] = np.where(rel <= 128, -sl * rel, -30000.0)
            rel = a_ - b_
            bt[:, g * 4 + hs, 128:256] = np.where(rel >= 0, -sl * rel, -30000.0)
    c["biasT"] = bt.astype(np.float32)
    return c


_NC_CACHE = {}


def kernel(x, c, w_ada, b_ada, g_norm1, g_norm2, g_final, w_in, b_gate, w_proj_sb, w_proj_dil, w_out,
           w_ffn_gate, w_ffn_up, w_ffn_down):
    f = lambda a: np.ascontiguousarray(np.asarray(a, dtype=np.float32))
    x = f(x)
    c = f(c)
    colT = lambda vec, n: np.ascontiguousarray(f(vec).reshape(n, 128).T)
    shared = {
        "w_ada": f(w_ada)[0], "b_ada": f(b_ada)[0], "b_adaT": colT(f(b_ada)[0], 96),
        "g1T": colT(f(g_norm1)[0], 16), "g2T": colT(f(g_norm2)[0], 16),
        "gfinal_b": np.ascontiguousarray(np.broadcast_to(f(g_final)[None, :], (128, D))),
        "w_in": f(w_in)[0], "b_gateT": colT(f(b_gate)[0], 32),
        "w_proj_sb": f(w_proj_sb)[0], "w_proj_dil": f(w_proj_dil)[0], "w_out": f(w_out)[0],
        "w_ffn_gate": f(w_ffn_gate)[0], "w_ffn_up": f(w_ffn_up)[0], "w_ffn_down": f(w_ffn_down)[0],
    }
    consts = [host_consts(0), host_consts(1)]
    in_maps = []
    for core in range(8):
        b, r = core // 2, core % 2
        if r == 1:
            xl = x[b]
        else:
            xl = np.ascontiguousarray(np.concatenate([x[b, OWN:], x[b, :OWN]], axis=0))
        m = dict(shared)
        m.update(consts[r])
        m["x"] = xl
        m["c_col"] = colT(c[b], 16)
        in_maps.append(m)
    if "nc" not in _NC_CACHE:
        _NC_CACHE["nc"] = build_program()
    nc = _NC_CACHE["nc"]
    res = run_bass_kernel_spmd(nc, in_maps, core_ids=list(range(8)))
    out = np.empty((4, S, D), np.float32)
    for core in range(8):
        b, r = core // 2, core % 2
        out[b, r * OWN:(r + 1) * OWN] = res.results[core]["out"]
    if DEBUG:
        kernel.last = res
    return out
```

```python
import numpy as np
from contextlib import ExitStack
import concourse.bass as bass
import concourse.mybir as mybir
from concourse.bass_utils import run_bass_kernel_spmd

F32 = mybir.dt.float32
BF16 = mybir.dt.bfloat16
AF = mybir.ActivationFunctionType
ALU = mybir.AluOpType

D = 2048
S = 2048
OWN = 1024
NCH = 16
DFF = 5632
NFF = DFF // 128
EPS = 1e-6
QS = 128 ** -0.5
DEBUG = False
STOP = 99


class StopBuild(Exception):
    pass


class T:
    def __init__(self, ap=None, name=""):
        self.ap = ap
        self.w = {}
        self.r = {}
        self.name = name
        self.dkey = None
        self.dcnt = 0
        self.excl = False


class Prog:
    ENG = ("pe", "act", "dve", "pool", "sp")

    def __init__(self, nc, es):
        self.nc = nc
        self.es = es
        self.q = {e: [] for e in self.ENG}
        self.sems = {}
        self.cnt = {}
        self.seen = {e: {} for e in self.ENG}
        for e in self.ENG:
            self.sems[e] = es.enter_context(nc.semaphore("s_" + e))
            self.cnt[e] = 0
        self.nd = 0
        self.rr = 0
        self.log = {e: [] for e in self.ENG}

    def dma_sem(self, t):
        if t.dkey is None:
            t.dkey = "d%d" % self.nd
            self.nd += 1
            self.sems[t.dkey] = self.es.enter_context(self.nc.semaphore(t.dkey))
        return t.dkey

    def _waits(self, e, deps):
        for key, val in deps.items():
            if self.seen[e].get(key, 0) >= val:
                continue
            self.seen[e][key] = val
            sem = self.sems[key]
            self.q[e].append(lambda eng, sem=sem, val=val: eng.wait_ge(sem, val))
            self.log[e].append(("wait", key, val))

    @staticmethod
    def _merge(dst, src):
        for k, v in src.items():
            if dst.get(k, 0) < v:
                dst[k] = v

    def _deps(self, e, reads, writes, pwrites):
        deps = {}
        for t in reads:
            self._merge(deps, t.w)
            if t.excl:
                for k, v in t.r.items():
                    if k != e and deps.get(k, 0) < v:
                        deps[k] = v
        for t in writes:
            for k, v in list(t.w.items()) + list(t.r.items()):
                if k != e and deps.get(k, 0) < v:
                    deps[k] = v
        for t in pwrites:
            for k, v in t.r.items():
                if k != e and deps.get(k, 0) < v:
                    deps[k] = v
        return deps

    def op(self, e, fn, reads=(), writes=(), pwrites=(), inc=True):
        self._waits(e, self._deps(e, reads, writes, pwrites))
        if inc:
            self.cnt[e] += 1
            sem = self.sems[e]
            self.q[e].append(lambda eng, fn=fn, sem=sem: fn(eng).then_inc(sem, 1))
            tok = self.cnt[e]
            self.log[e].append(("inc", e, 1))
        else:
            self.q[e].append(lambda eng, fn=fn: fn(eng))
            tok = self.cnt[e] + 1
        for t in reads:
            if t.r.get(e, 0) < tok:
                t.r[e] = tok
        for t in writes:
            t.w = {e: tok}
        for t in pwrites:
            if t.w.get(e, 0) < tok:
                t.w[e] = tok

    def dma(self, e, out_ap, in_ap, reads=(), writes=(), pwrites=(), semt=None, noncontig=False):
        self._waits(e, self._deps(e, reads, writes, pwrites))
        st = semt if semt is not None else (list(writes) + list(pwrites) + list(reads))[0]
        key = self.dma_sem(st)
        st.dcnt += 16
        sem = self.sems[key]
        nc = self.nc
        if noncontig:
            def f(eng, o=out_ap, i=in_ap, sem=sem):
                with nc.allow_non_contiguous_dma(reason="small layout shuffle"):
                    eng.dma_start(out=o, in_=i).then_inc(sem, 16)
        else:
            def f(eng, o=out_ap, i=in_ap, sem=sem):
                eng.dma_start(out=o, in_=i).then_inc(sem, 16)
        self.q[e].append(f)
        self.log[e].append(("inc", key, 16))
        tok = st.dcnt
        for t in reads:
            if t.r.get(key, 0) < tok:
                t.r[key] = tok
        for t in writes:
            t.w = {key: tok}
        for t in pwrites:
            if t.w.get(key, 0) < tok:
                t.w[key] = tok

    def handoff(self, new, olds):
        for o in olds:
            self._merge(new.r, o.r)
            self._merge(new.r, o.w)

    def final_wait(self, e, ts):
        deps = {}
        for t in ts:
            self._merge(deps, t.w)
            self._merge(deps, t.r)
            if t.dkey is not None:
                self._merge(deps, {t.dkey: t.dcnt})
        self._waits(e, deps)


def build_program():
    nc = bass.Bass("TRN2", target_bir_lowering=False)
    es = ExitStack()
    P = Prog(nc, es)

    def din(name, shape):
        return nc.dram_tensor(name, list(shape), F32, kind="ExternalInput").ap()

    x_d = din("x", [S, D])
    ccol_d = din("c_col", [128, 16])
    wada_d = din("w_ada", [D, 6 * D])
    bada_d = din("b_ada", [6 * D])
    badaT_d = din("b_adaT", [128, 96])
    g1T_d = din("g1T", [128, 16])
    g2T_d = din("g2T", [128, 16])
    gfb_d = din("gfinal_b", [128, D])
    win_d = din("w_in", [D, 11776])
    bgT_d = din("b_gateT", [128, 32])
    wps_d = din("w_proj_sb", [1024, D])
    wpd_d = din("w_proj_dil", [512, D])
    wout_d = din("w_out", [D, D])
    wg_d = din("w_ffn_gate", [D, DFF])
    wu_d = din("w_ffn_up", [D, DFF])
    wd_d = din("w_ffn_down", [DFF, D])
    ident_d = din("ident", [128, 128])
    negtri_d = din("negtri", [128, 128])
    negrest_d = din("negrest", [128, 128])
    dmask_d = din("dmask", [128, 128])
    vones_d = din("vones", [128, 3, 128])
    vcol_d = din("vcol", [128, 3])
    biasT_d = din("biasT", [128, 12, 256])
    out_d = nc.dram_tensor("out", [OWN, D], F32, kind="ExternalOutput").ap()
    scr_d = nc.dram_tensor("scr_mod", [6 * D], F32).ap()
    if DEBUG:
        xnew_d = nc.dram_tensor("xnew", [OWN, D], F32, kind="ExternalOutput").ap()
        dbg_hT = nc.dram_tensor("dbg_hT", [128, NCH, S], F32, kind="ExternalOutput").ap()
        dbg_osb = nc.dram_tensor("dbg_osb", [128, 8, OWN], F32, kind="ExternalOutput").ap()
        dbg_odl = nc.dram_tensor("dbg_odl", [128, 4, OWN], F32, kind="ExternalOutput").ap()
        dbg_mT = nc.dram_tensor("dbg_mT", [128, NCH, OWN], F32, kind="ExternalOutput").ap()
        dbg_h2T = nc.dram_tensor("dbg_h2T", [128, NCH, OWN], F32, kind="ExternalOutput").ap()
        dbg_mod = nc.dram_tensor("dbg_mod", [128, 96], F32, kind="ExternalOutput").ap()
    else:
        xnew_d = nc.dram_tensor("xnew", [OWN, D], F32).ap()

    def sb(name, shape, dt):
        return es.enter_context(nc.sbuf_tensor("sb_" + name, list(shape), dt))

    ident = sb("ident", [128, 128], BF16)
    negtri = sb("negtri", [128, 128], BF16)
    negrest = sb("negrest", [128, 128], BF16)
    dmask = sb("dmask", [128, 128], F32)
    vones = sb("vones", [128, 3, 128], BF16)
    vcol = sb("vcol", [128, 3], F32)
    ccol = sb("ccol", [128, 16], F32)
    sc = sb("sc", [128, 16], BF16)
    badaT = sb("badaT", [128, 96], F32)
    modT = sb("modT", [128, 96], F32)
    g1T = sb("g1T", [128, 16], F32)
    g2T = sb("g2T", [128, 16], F32)
    bgT = sb("bgT", [128, 32], F32)
    a1 = sb("a1", [128, 16], F32)
    a2 = sb("a2", [128, 16], F32)
    ss = sb("ss", [128, 8], F32)
    rstd = sb("rstd", [128, 8], F32)
    rowt = sb("rowt", [1, 2, 256], F32)
    browt = sb("browt", [1, 2, 256], F32)
    biasT = sb("biasT", [128, 2, 256], F32)
    onec = sb("onec", [128, 1], F32)
    M = sb("M", [128, 61440], BF16)
    hTb_ap = M[:, 0:16384].rearrange("p (c t) -> p c t", c=16)
    hTa_ap = M[:, 16384:32768].rearrange("p (c t) -> p c t", c=16)
    r1 = M[:, 32768:49152]
    Mf = M[:].bitcast(F32)
    odl = M[:, 49152:53248].rearrange("p (h t) -> p h t", h=4)
    osb = M[:, 53248:61440].rearrange("p (h t) -> p h t", h=8)
    v2b_sb = sb("v2b", [128, 4096], BF16)
    acc = sb("acc", [128, 4096], F32)
    work = sb("work", [128, 3584], F32)
    NRING = 5
    ring = sb("ring", [128, NRING, 4096], BF16)
    ps = [es.enter_context(nc.psum_tensor("ps%d" % i, [128, 512], F32)) for i in range(8)]

    Tc = T(name="consts")
    bank = [T(ps[i], "bank%d" % i) for i in range(8)]
    for t_ in bank:
        t_.excl = True
    ringT = [T(ring[:, i, :], "ring%d" % i) for i in range(NRING)]
    ring_i = [0]

    NRING_BOX = [NRING]

    def ring_next():
        t = ringT[ring_i[0] % NRING_BOX[0]]
        ring_i[0] += 1
        return t

    hTt = [T(name="hT%d" % i) for i in range(4)]

    def hsl(k, t0, t1):
        if t0 >= OWN:
            return hTb_ap[:, k, t0 - OWN:t1 - OWN]
        return hTa_ap[:, k, t0:t1]

    xt_t = [T(Mf[:, 16384 + 2048 * i:16384 + 2048 * (i + 1)], "xt%d" % i) for i in range(2)]
    xn_t = [T(r1[:, 8192 + 4096 * i: 8192 + 4096 * (i + 1)], "xn%d" % i) for i in range(2)]
    Q2 = T(r1[:, 0:2048].rearrange("p (j t) -> p j t", j=2), "Q2")
    K2 = T(r1[:, 2048:6144].rearrange("p (j t) -> p j t", j=2), "K2")
    V2 = T(r1[:, 6144:10240].rearrange("p (b c) -> p b c", b=16), "V2")
    Q2b = T(r1[:, 10240:12288].rearrange("p (j t) -> p j t", j=2), "Q2b")
    K2b = T(r1[:, 12288:16384].rearrange("p (j t) -> p j t", j=2), "K2b")
    V2b = T(v2b_sb[:].rearrange("p (b c) -> p b c", b=16), "V2b")
    QKV = [(Q2, K2, V2), (Q2b, K2b, V2b)]
    osbT = T(osb, "osb")
    odlT = T(odl, "odl")
    accn = T(acc[:, 0:2048].rearrange("p (j t) -> p j t", j=2), "accn")
    accd = T(acc[:, 2048:4096].rearrange("p (j t) -> p j t", j=2), "accd")
    g1b = T(acc[:, 0:2048], "g1b")
    g2b = T(acc[:, 2048:4096], "g2b")
    gfb = T(acc[:, 0:2048], "gfb")
    eT = [T(work[:, 512 * i:512 * (i + 1)], "e%d" % i) for i in range(3)]
    ecT = [T(work[:, 1536 + 512 * i:1536 + 512 * (i + 1)], "ec%d" % i) for i in range(2)]
    wbf = work[:].bitcast(BF16)[:, 5120:7168]
    spT = [T(wbf[:, 512 * i:512 * (i + 1)], "sp%d" % i) for i in range(2)]
    aT = [T(wbf[:, 1024 + 512 * i:1024 + 512 * (i + 1)], "a%d" % i) for i in range(2)]
    mT = T(hTa_ap, "mT")
    actT = T(M[:, 16384:61440].rearrange("p (c t) -> p c t", c=NFF), "actT")
    h2Tt = [T(name="h2T%d" % i) for i in range(2)]
    h2T = hTb_ap
    zxt = [T(Mf[:, 2048 * i:2048 * (i + 1)], "zxt%d" % i) for i in range(2)]
    zov = [T(Mf[:, 4096 + 2048 * i:4096 + 2048 * (i + 1)], "zov%d" % i) for i in range(2)]
    modTt = T(modT, "modT")
    a1t, a2t = T(a1, "a1"), T(a2, "a2")
    sst, rstdt = T(ss, "ss"), T(rstd, "rstd")
    sct = T(sc, "sc")
    rowT = [T(rowt[:, i, :], "row%d" % i) for i in range(2)]
    browT = [T(browt[:, i, :], "brow%d" % i) for i in range(2)]
    biasTt = T(biasT, "biasT")
    scrT = T(name="scr")
    xnewT = [T(name="xnew%d" % i) for i in range(8)]
    outT = [T(name="out%d" % i) for i in range(8)]

    DBGT = []

    def dbgT(name):
        t = T(name=name)
        DBGT.append(t)
        return t

    def mm(out_ap, lhsT, rhs, start, stop, reads, bankt, inc):
        P.op("pe", lambda e: e.matmul(out_ap, lhsT=lhsT, rhs=rhs, start=start, stop=stop, skip_group_check=True),
             reads=reads, pwrites=[bankt], inc=inc)

    def evac(out_ap, in_ap, reads, writes=(), pwrites=(), scale=None, bias=None, eng=None):
        if eng is None:
            eng = ("act", "dve")[P.rr % 2]
            P.rr += 1
        if eng == "act":
            kw = {}
            if scale is not None:
                kw["scale"] = scale
            if bias is not None:
                kw["bias"] = bias
            P.op("act", lambda e: e.activation(out=out_ap, in_=in_ap, func=AF.Identity, **kw),
                 reads=reads, writes=writes, pwrites=pwrites)
        else:
            if scale is None and bias is None:
                P.op("dve", lambda e: e.tensor_copy(out=out_ap, in_=in_ap), reads=reads, writes=writes, pwrites=pwrites)
            elif bias is None:
                P.op("dve", lambda e: e.tensor_scalar(out=out_ap, in0=in_ap, scalar1=scale, scalar2=None, op0=ALU.mult),
                     reads=reads, writes=writes, pwrites=pwrites)
            else:
                s1 = scale if scale is not None else 1.0
                P.op("dve", lambda e: e.tensor_scalar(out=out_ap, in0=in_ap, scalar1=s1, scalar2=bias,
                                                      op0=ALU.mult, op1=ALU.add),
                     reads=reads, writes=writes, pwrites=pwrites)

    gb = [0]
    GEN_BANKS = [2, 3]

    def next_bank(pool=None):
        pool = pool or GEN_BANKS
        b = pool[gb[0] % len(pool)]
        gb[0] += 1
        return b

    def wpanel(src_ap, nk):
        t = ring_next()
        dst = t.ap[:, 0:nk * 256].rearrange("p (k n) -> p k n", k=nk)
        P.dma("pool", dst, src_ap.rearrange("(k p) n -> p k n", p=128), writes=[t])
        return t, dst

    def stop(n):
        if STOP == n:
            raise StopBuild()

    try:
        def cload(q, dst, src):
            P.dma(q, dst, src, pwrites=[Tc], semt=Tc)

        cload("pool", ident[:], ident_d)
        cload("pool", negtri[:], negtri_d)
        cload("pool", negrest[:], negrest_d)
        cload("pool", vones[:], vones_d)
        cload("sp", dmask[:], dmask_d)
        cload("sp", vcol[:], vcol_d)
        cload("sp", ccol[:], ccol_d)
        cload("sp", badaT[:], badaT_d)
        cload("sp", g1T[:], g1T_d)
        cload("sp", g2T[:], g2T_d)
        cload("sp", bgT[:], bgT_d)

        P.op("act", lambda e: e.activation(out=sc[:], in_=ccol[:], func=AF.Silu), reads=[Tc], writes=[sct])
        onect = T(onec, "onec")
        P.op("dve", lambda e: e.memset(onec[:], 1.0), writes=[onect])

        PM = 3
        mod_done = [0]

        def mod_panel(pi):
            t, w = wpanel(wada_d[:, pi * 256:(pi + 1) * 256], 16)
            kind = (pi * 256) // D
            if kind in (2, 5):
                b = next_bank()
                for k in range(16):
                    mm(ps[b][0:1, 0:256], sc[:, k:k + 1], w[:, k, :], k == 0, k == 15, [t, sct], bank[b], k == 15)
                i = mod_done[0] % 2
                mod_done[0] += 1
                P.dma("sp", browt[:, i, :], bada_d[pi * 256:(pi + 1) * 256].rearrange("(o n) -> o n", o=1), writes=[browT[i]])
                P.op("dve", lambda e: e.tensor_tensor(out=rowt[:, i, :], in0=ps[b][0:1, 0:256], in1=browt[:, i, :], op=ALU.add),
                     reads=[bank[b], browT[i]], writes=[rowT[i]])
                P.dma("sp", scr_d[pi * 256:(pi + 1) * 256].rearrange("(o n) -> o n", o=1), rowt[:, i, :],
                      reads=[rowT[i]], pwrites=[scrT], semt=rowT[i])
            else:
                b = next_bank()
                for j in range(2):
                    for k in range(16):
                        mm(ps[b][:, j:j + 1], w[:, k, j * 128:(j + 1) * 128], sc[:, k:k + 1],
                           k == 0, k == 15, [t, sct], bank[b], k == 15)
                nch = pi * 2
                P.op("dve", lambda e: e.tensor_tensor(out=modT[:, nch:nch + 2], in0=ps[b][:, 0:2], in1=badaT[:, nch:nch + 2],
                                                      op=ALU.add),
                     reads=[bank[b], Tc], pwrites=[modTt])

        stop(-1)
        TPB = [4, 5, 6, 7]
        tpi = [0]

        def norm_T(pair_i, xts, dstT_ap, dst_t, at, a_ap, b_ap, b_t, col0, xn=None):
            xn = xn if xn is not None else xn_t[pair_i % 2]
            xnv = xn.ap.rearrange("p (u n) -> p u n", u=2)
            for u in range(2):
                xt = xts[u]
                si = (pair_i % 4) * 2 + u
                P.op("act", lambda e, xt=xt, u=u, si=si: e.activation(out=xnv[:, u, :], in_=xt.ap, func=AF.Square,
                                                                      accum_out=ss[:, si:si + 1]),
                     reads=[xt], pwrites=[xn, sst])
                P.op("dve", lambda e, si=si: e.tensor_scalar(out=rstd[:, si:si + 1], in0=ss[:, si:si + 1], scalar1=1.0 / D,
                                                             scalar2=EPS, op0=ALU.mult, op1=ALU.add),
                     reads=[sst], pwrites=[rstdt])
                P.op("act", lambda e, si=si: e.activation(out=rstd[:, si:si + 1], in_=rstd[:, si:si + 1], func=AF.Sqrt),
                     reads=[rstdt], pwrites=[rstdt])
                P.op("dve", lambda e, si=si: e.reciprocal(out=rstd[:, si:si + 1], in_=rstd[:, si:si + 1]),
                     reads=[rstdt], pwrites=[rstdt])
                P.op("dve", lambda e, xt=xt, u=u, si=si: e.tensor_scalar(out=xnv[:, u, :], in0=xt.ap,
                                                                        scalar1=rstd[:, si:si + 1], scalar2=None, op0=ALU.mult),
                     reads=[xt, rstdt], pwrites=[xn])
            for c4 in range(4):
                b = TPB[tpi[0] % len(TPB)]
                tpi[0] += 1
                pv = ps[b][:].bitcast(BF16)
                for cc in range(4):
                    c = c4 * 4 + cc
                    for u in range(2):
                        last = (cc == 3 and u == 1)
                        o = pv[:, cc * 256 + u * 128: cc * 256 + (u + 1) * 128]
                        i_ = xnv[:, u, c * 128:(c + 1) * 128]
                        P.op("pe", lambda e, o=o, i_=i_: e.transpose(o, i_, ident[:]),
                             reads=[xn, Tc], pwrites=[bank[b]], inc=last)
                for cc in range(4):
                    c = c4 * 4 + cc
                    if at is None:
                        evac(dstT_ap[:, c, col0:col0 + 256], pv[:, cc * 256:(cc + 1) * 256], reads=[bank[b]],
                             pwrites=[dst_t], eng=("act", "dve")[c4 % 2])
                    else:
                        evac(dstT_ap[:, c, col0:col0 + 256], pv[:, cc * 256:(cc + 1) * 256], reads=[bank[b], at, b_t],
                             pwrites=[dst_t], scale=a_ap[:, c:c + 1], bias=b_ap[:, c:c + 1], eng=("act", "dve")[c4 % 2])

        hTraw = [T(name="hTraw%d" % i) for i in range(4)]
        for pr in range(8):
            mod_panel(2 * pr)
            mod_panel(2 * pr + 1)
            xts = []
            for u in range(2):
                xt = xt_t[u]
                tb = pr * 2 + u
                P.dma("sp", xt.ap, x_d[tb * 128:(tb + 1) * 128, :], writes=[xt])
                xts.append(xt)
            tg = pr // 2
            norm_T(pr, xts, hTa_ap if pr < 4 else hTb_ap, hTraw[tg], None, None, None, None, (pr % 4) * 256)
        P.op("dve", lambda e: e.scalar_tensor_tensor(out=a1[:], in0=modT[:, 16:32], scalar=1.0, in1=g1T[:],
                                                     op0=ALU.add, op1=ALU.mult),
             reads=[modTt, Tc], writes=[a1t])
        if DEBUG:
            P.dma("sp", dbg_mod, modT[:], reads=[modTt], semt=dbgT("dbg0a"))
        stop(0)
        for tg in (2, 3, 0, 1):
            eng = ("act", "dve")[tg % 2]
            for c in range(16):
                ap_ = hsl(c, tg * 512, (tg + 1) * 512)
                if eng == "act":
                    P.op("act", lambda e, ap_=ap_, c=c: e.activation(out=ap_, in_=ap_, func=AF.Identity,
                                                                    scale=a1[:, c:c + 1], bias=modT[:, c:c + 1]),
                         reads=[hTraw[tg], a1t, modTt], pwrites=[hTt[tg]])
                else:
                    P.op("dve", lambda e, ap_=ap_, c=c: e.tensor_scalar(out=ap_, in0=ap_, scalar1=a1[:, c:c + 1],
                                                                       scalar2=modT[:, c:c + 1], op0=ALU.mult, op1=ALU.add),
                         reads=[hTraw[tg], a1t, modTt], pwrites=[hTt[tg]])

        if DEBUG:
            P.dma("pool", dbg_hT[:, :, 0:OWN], hTa_ap, reads=hTt, semt=dbgT("dbg1"))
            P.dma("pool", dbg_hT[:, :, OWN:S], hTb_ap, reads=hTt, semt=dbgT("dbg1b"))
            P.dma("sp", dbg_mod, modT[:], reads=[modTt], semt=dbgT("dbg0"))

        stop(1)
        for t in (Q2, K2, V2, Q2b, K2b):
            P.handoff(t, xt_t + xn_t)

        def project_thunks(qc0, kc0, vc0, d, qkv, ev_eng):
            Q2_, K2_, V2_ = qkv
            st = {}
            th = []
            pend = []

            def flush(keep):
                while len(pend) > keep:
                    pend.pop(0)()

            def load():
                st["q"] = wpanel(win_d[:, qc0:qc0 + 256], 16)
                st["k"] = wpanel(win_d[:, kc0:kc0 + 256], 16)
                st["v"] = wpanel(win_d[:, vc0:vc0 + 256], 16)
            th.append(load)

            def fm_group(which, dstT, j, tg, t0, scale, part, hold):
                tq, wq = st[which]
                if part == 0:
                    hold["b"] = next_bank()
                b = hold["b"]
                for k in range(part * 4, part * 4 + 4):
                    mm(ps[b][:, :], wq[:, k, j * 128:(j + 1) * 128], hsl(k, t0, t0 + 512),
                       k == 0, k == 15, [tq, hTt[t0 // 512]], bank[b], k == 15)
                if part < 3:
                    return
                if d == 1:
                    dst = dstT.ap[:, j, tg * 512:(tg + 1) * 512]
                    src = ps[b][:, :]
                else:
                    ni = 512 // d
                    dst = dstT.ap[:, j, :].rearrange("p (r i) -> p r i", r=d)[:, :, tg * ni:(tg + 1) * ni]
                    src = ps[b][:, :].rearrange("p (i r) -> p r i", r=d)
                pend.append(lambda: evac(dst, src, reads=[bank[b]], pwrites=[dstT], scale=scale, eng=ev_eng))

            for j in range(2):
                for tg in range(2):
                    hold = {}
                    for part in range(4):
                        th.append(lambda j=j, tg=tg, part=part, hold=hold: fm_group("q", Q2_, j, tg, OWN + tg * 512, QS, part, hold))
            for j in range(2):
                for tg in range(4):
                    hold = {}
                    for part in range(4):
                        th.append(lambda j=j, tg=tg, part=part, hold=hold: fm_group("k", K2_, j, tg, tg * 512, None, part, hold))

            def v_group(tb, part, hold):
                tv, wv = st["v"]
                if d == 1:
                    parts = [(slice(0, 128), lambda k: hsl(k, tb * 128, (tb + 1) * 128))]
                    rd = [hTt[tb // 4]]
                    vk = 0 if tb < 8 else 1
                elif d == 4:
                    r, ib = tb // 4, tb % 4
                    parts = [(slice(0, 128), lambda k: hsl(k, ib * 512, (ib + 1) * 512).rearrange(
                        "p (i r) -> p r i", r=4)[:, r, :])]
                    rd = [hTt[ib]]
                    vk = 0 if ib < 2 else 1
                else:
                    r = tb
                    parts = [(slice(0, 64), lambda k: hTa_ap[:, k, :].rearrange("p (i r) -> p r i", r=16)[:, r, :]),
                             (slice(64, 128), lambda k: hTb_ap[:, k, :].rearrange("p (i r) -> p r i", r=16)[:, r, :])]
                    rd = hTt
                    vk = 2
                if part == 0:
                    hold["b"] = next_bank()
                b = hold["b"]
                allmm = [(pi_, k) for pi_ in range(len(parts)) for k in range(16)]
                q4 = len(allmm) // 4
                for (pi_, k) in allmm[part * q4:(part + 1) * q4]:
                    psl, tsel = parts[pi_]
                    mm(ps[b][psl, 0:256], tsel(k), wv[:, k, :], k == 0, k == 15, [tv] + list(rd), bank[b],
                       k == 15 and pi_ == len(parts) - 1)
                if part == 3:
                    pend.append(lambda: evac(V2_.ap[:, tb, :], ps[b][:, 0:256], reads=[bank[b], Tc], pwrites=[V2_],
                                             scale=vcol[:, vk:vk + 1], eng=ev_eng))

            for tb in range(16):
                hold = {}
                for part in range(4):
                    th.append(lambda tb=tb, part=part, hold=hold: v_group(tb, part, hold))

            def wrap(f, last):
                def g():
                    cnt_before = len(pend)
                    f()
                    if len(pend) > cnt_before:
                        pend[-1].age = 0
                    for p_ in list(pend):
                        p_.age = getattr(p_, 'age', 0) + 1
                    while pend and (pend[0].age > 2 or last):
                        pend.pop(0)()
                return g
            return [wrap(f, i == len(th) - 1) for i, f in enumerate(th)]

        def sb_thunks(hp, qkv):
            Q2_, K2_, V2_ = qkv
            ZB = [0, 1]
            seq = []
            for j in range(2):
                streams = []
                for qc in range(2):
                    qb0 = 8 + 4 * qc
                    tiles = []
                    for kb in range(qb0 + 3, -1, -1):
                        c0 = max(0, kb - qb0) * 128
                        tiles.append((j, hp * 2 + j, qc, kb, c0, kb >= qb0))
                    streams.append(tiles)
                for i in range(max(len(s_) for s_ in streams)):
                    for s_ in streams:
                        if i < len(s_):
                            seq.append(s_[i])
            Pb = {0: 4, 1: 6}
            Ob = {0: 5, 1: 7}
            first = {(0, 0): True, (0, 1): True, (1, 0): True, (1, 1): True}
            st = {}

            n = len(seq)

            def zmm(i):
                j, head, qc, kb, c0, diag = seq[i]
                N = 512 - c0
                zb = ZB[i % 2]
                mm(ps[zb][:, 0:N], K2_.ap[:, j, kb * 128:(kb + 1) * 128], Q2_.ap[:, j, qc * 512 + c0:(qc + 1) * 512],
                   True, True, [K2_, Q2_], bank[zb], True)

            def expz(i):
                j, head, qc, kb, c0, diag = seq[i]
                N = 512 - c0
                zb = ZB[i % 2]
                e_ = eT[i % 3]
                P.op("act", lambda e: e.activation(out=e_.ap[:, 0:N], in_=ps[zb][:, 0:N], func=AF.Exp),
                     reads=[bank[zb]], writes=[e_])
                if diag:
                    P.op("dve", lambda e: e.tensor_tensor(out=e_.ap[:, 0:128], in0=e_.ap[:, 0:128], in1=dmask[:], op=ALU.mult),
                         reads=[e_, Tc], pwrites=[e_])

            def ln(i):
                j, head, qc, kb, c0, diag = seq[i]
                N = 512 - c0
                e_, sp_ = eT[i % 3], spT[i % 2]
                vk = 0 if kb < 8 else 1
                P.op("act", lambda e: e.activation(out=sp_.ap[:, 0:N], in_=e_.ap[:, 0:N], func=AF.Ln, bias=onec[:, 0:1],
                                                   scale=vcol[:, vk:vk + 1]),
                     reads=[e_, Tc, onect], writes=[sp_])

            def tri(i):
                j, head, qc, kb, c0, diag = seq[i]
                N = 512 - c0
                sp_ = spT[i % 2]
                fst = first[(j, qc)]
                first[(j, qc)] = False
                st[i] = fst
                mm(ps[Pb[qc]][:, c0:512], negtri[:], sp_.ap[:, 0:N], fst, True, [sp_, Tc], bank[Pb[qc]], True)

            def expc(i):
                j, head, qc, kb, c0, diag = seq[i]
                N = 512 - c0
                ec_ = ecT[i % 2]
                P.op("act", lambda e: e.activation(out=ec_.ap[:, 0:N], in_=ps[Pb[qc]][:, c0:512], func=AF.Exp),
                     reads=[bank[Pb[qc]]], writes=[ec_])

            def rest(i):
                j, head, qc, kb, c0, diag = seq[i]
                N = 512 - c0
                sp_ = spT[i % 2]
                if kb > 0:
                    mm(ps[Pb[qc]][:, c0:512], negrest[:], sp_.ap[:, 0:N], False, True, [sp_, Tc], bank[Pb[qc]], True)

            def av(i):
                j, head, qc, kb, c0, diag = seq[i]
                N = 512 - c0
                e_, ec_, a_ = eT[i % 3], ecT[i % 2], aT[i % 2]
                ob = Ob[qc]
                P.op("dve", lambda e: e.tensor_tensor(out=a_.ap[:, 0:N], in0=e_.ap[:, 0:N], in1=ec_.ap[:, 0:N], op=ALU.mult),
                     reads=[e_, ec_], writes=[a_])
                mm(ps[ob][:, c0:512], V2_.ap[:, kb, j * 128:(j + 1) * 128], a_.ap[:, 0:N], st[i], True, [V2_, a_], bank[ob], True)
                if kb == 0:
                    evac(osb[:, head, qc * 512:(qc + 1) * 512], ps[ob][:, :], reads=[bank[ob]], pwrites=[osbT])

            def it(i):
                if i == -2:
                    zmm(0)
                if 0 <= i + 2 < n:
                    expz(i + 2)
                if 0 <= i + 1 < n:
                    ln(i + 1)
                if i >= 0:
                    expc(i)
                    rest(i)
                if 0 <= i + 1 < n:
                    tri(i + 1)
                if 0 <= i + 3 < n:
                    zmm(i + 3)
                if i >= 0:
                    av(i)

            th = [lambda i=i: it(i) for i in range(-2, n)]
            return th

        def dil_thunks(g, d, first_group, qkv):
            Q2_, K2_, V2_ = qkv
            L = S // d
            nb = max(1, L // 128)
            DB = [0, 1, 4, 5, 6, 7]
            cnt = [0]
            th = []

            units = []
            for j in range(2):
                for r in range(d):
                    blocks = [None] if d == 16 else list(range(nb // 2, nb))
                    for qb in blocks:
                        units.append((j, r, qb))
            info = {}

            def s1(u):
                j, r, qb = units[u]
                zb = DB[u % 3]
                w = u % 2
                if d == 16:
                    N = 64
                    mm(ps[zb][:, 0:64], K2_.ap[:, j, r * 128:(r + 1) * 128], Q2_.ap[:, j, r * 64:(r + 1) * 64],
                       True, True, [K2_, Q2_], bank[zb], True)
                    bsl = biasT[:, j, 192:256]
                    kinds = [2]
                    vblocks = [r]
                    accv = lambda A: A.ap[:, j, :].rearrange("p (i r) -> p r i", r=16)[:, r, 0:64]
                    NQ = 64
                else:
                    N = 256
                    nbl = L // 128
                    kprev = r * nbl + qb - 1
                    kcur = r * nbl + qb
                    qoff = r * (L // 2) + (qb - nb // 2) * 128
                    mm(ps[zb][:, 0:128], K2_.ap[:, j, kprev * 128:(kprev + 1) * 128], Q2_.ap[:, j, qoff:qoff + 128],
                       True, True, [K2_, Q2_], bank[zb], False)
                    mm(ps[zb][:, 128:256], K2_.ap[:, j, kcur * 128:(kcur + 1) * 128], Q2_.ap[:, j, qoff:qoff + 128],
                       True, True, [K2_, Q2_], bank[zb], True)
                    bsl = biasT[:, j, 0:256]
                    half = nb // 2
                    kinds = [0 if (qb - 1) < half else 1, 0 if qb < half else 1]
                    vblocks = [kprev, kcur]
                    i0 = (qb - nb // 2) * 128
                    if d == 1:
                        accv = lambda A: A.ap[:, j, i0:i0 + 128]
                    else:
                        accv = lambda A: A.ap[:, j, :].rearrange("p (i r) -> p r i", r=d)[:, r, i0:i0 + 128]
                    NQ = 128
                s_, p_ = eT[w], aT[w]
                P.op("dve", lambda e: e.tensor_tensor(out=s_.ap[:, 0:N], in0=ps[zb][:, 0:N], in1=bsl, op=ALU.add),
                     reads=[bank[zb], biasTt], writes=[s_])
                P.op("act", lambda e: e.activation(out=p_.ap[:, 0:N], in_=s_.ap[:, 0:N], func=AF.Exp),
                     reads=[s_], writes=[p_])
                info[u] = (p_, kinds, vblocks, accv, NQ, j)

            def s2(u):
                p_, kinds, vblocks, accv, NQ, j = info[u]
                nbk = DB[3 + u % 3]
                nk = len(kinds)
                for i in range(nk):
                    mm(ps[nbk][:, 0:NQ], V2_.ap[:, vblocks[i], j * 128:(j + 1) * 128], p_.ap[:, i * NQ:(i + 1) * NQ],
                       i == 0, i == nk - 1, [V2_, p_], bank[nbk], False)
                for i in range(nk):
                    mm(ps[nbk][:, 128:128 + NQ], vones[:, kinds[i], :], p_.ap[:, i * NQ:(i + 1) * NQ],
                       i == 0, i == nk - 1, [Tc, p_], bank[nbk], i == nk - 1)
                for (A, c0) in ((accn, 0), (accd, 128)):
                    av_ = accv(A)
                    if first_group:
                        P.op("dve", lambda e, av_=av_, c0=c0: e.tensor_copy(out=av_, in_=ps[nbk][:, c0:c0 + NQ]),
                             reads=[bank[nbk]], pwrites=[A])
                    else:
                        P.op("dve", lambda e, av_=av_, c0=c0: e.tensor_tensor(out=av_, in0=av_, in1=ps[nbk][:, c0:c0 + NQ], op=ALU.add),
                             reads=[bank[nbk], A], pwrites=[A])

            nu = len(units)

            def it(u):
                if u + 1 < nu:
                    s1(u + 1)
                if u >= 0:
                    s2(u)
            th = [lambda u=u: it(u) for u in range(-1, nu)]
            return th

        mod_next = [16]

        def mod_thunks(n):
            th = []
            for _ in range(n):
                if mod_next[0] < 48:
                    th.append(lambda pi=mod_next[0]: mod_panel(pi))
                    mod_next[0] += 1
            return th

        def finish_pair(pair):
            for j in range(2):
                hs = pair * 2 + j
                P.op("dve", lambda e, j=j: e.reciprocal(out=accd.ap[:, j, :], in_=accd.ap[:, j, :]),
                     reads=[accd], pwrites=[accd])
                P.op("dve", lambda e, j=j, hs=hs: e.tensor_tensor(out=odl[:, hs, :], in0=accn.ap[:, j, :], in1=accd.ap[:, j, :],
                                                                 op=ALU.mult),
                     reads=[accn, accd], pwrites=[odlT])

        DIL = ((128, 1), (512, 4), (2048, 16))
        rounds = []
        for hp in range(4):
            rounds.append(("sb", hp, (hp * 256, 1024 + hp * 256, 2048 + hp * 256, 1)))
        for pair in range(2):
            for g, (win, d) in enumerate(DIL):
                off = g * 512 + pair * 256
                rounds.append(("dil", (pair, g, d), (3072 + off, 4608 + off, 6144 + off, d)))

        def att_thunks(ri):
            kind, info, _ = rounds[ri]
            qkv = QKV[ri % 2]
            th = []
            if kind == "sb":
                th += sb_thunks(info, qkv)
            else:
                pair, g, d = info
                th.append(lambda: P.dma("sp", biasT[:], biasT_d[:, g * 4 + pair * 2: g * 4 + pair * 2 + 2, :], writes=[biasTt]))
                th += dil_thunks(g, d, g == 0, qkv)
                if g == 2:
                    th.append(lambda: finish_pair(pair))
            return th

        def proj_thunks(ri):
            _, _, (qc0, kc0, vc0, d) = rounds[ri]
            prev_kind = rounds[ri - 1][0] if ri > 0 else 'dil'
            return project_thunks(qc0, kc0, vc0, d, QKV[ri % 2], 'dve' if prev_kind == 'sb' else 'act') + mod_thunks(4)

        def interleave(A, B):
            na, nb_ = len(A), len(B)
            bi = 0
            for k, a_ in enumerate(A):
                a_()
                while bi < nb_ and (bi + 1) * na <= (k + 1) * nb_:
                    B[bi]()
                    bi += 1
            while bi < nb_:
                B[bi]()
                bi += 1

        for th_ in proj_thunks(0):
            th_()
        stop(2)
        for ri in range(len(rounds)):
            nxt = proj_thunks(ri + 1) if ri + 1 < len(rounds) else mod_thunks(48)
            interleave(att_thunks(ri), nxt)
            if ri == 0:
                if DEBUG:
                    P.dma("pool", dbg_osb, osb, reads=[osbT], semt=dbgT("dbg2a"))
                stop(3)
        for th_ in mod_thunks(48):
            th_()
        if DEBUG:
            P.dma("pool", dbg_osb, osb, reads=[osbT], semt=dbgT("dbg2"))
            P.dma("pool", dbg_odl, odl, reads=[odlT], semt=dbgT("dbg3"))

        ringT.append(T(v2b_sb[:], "ring_v2b"))
        P.handoff(ringT[-1], [V2b])
        NRING_BOX[0] = len(ringT)
        stop(5)
        P.op("dve", lambda e: e.scalar_tensor_tensor(out=a2[:], in0=modT[:, 64:80], scalar=1.0, in1=g2T[:],
                                                     op0=ALU.add, op1=ALU.mult),
             reads=[modTt, Tc], writes=[a2t])
        if DEBUG:
            P.dma("sp", dbg_mod, modT[:], reads=[modTt], semt=dbgT("dbg0c"))
        P.handoff(g1b, [accn, accd])
        P.handoff(g2b, [accn, accd])
        P.dma("sp", g1b.ap, scr_d[2 * D:3 * D].partition_broadcast(128), reads=[scrT], writes=[g1b])
        P.dma("sp", g2b.ap, scr_d[5 * D:6 * D].partition_broadcast(128), reads=[scrT], writes=[g2b])

        P.handoff(mT, hTt[0:2])
        for c2 in range(8):
            tgs, wgs = wpanel(win_d[:, 7680 + c2 * 256: 7680 + (c2 + 1) * 256], 16)
            tgd, wgd = wpanel(win_d[:, 9728 + c2 * 256: 9728 + (c2 + 1) * 256], 16)
            tp = ring_next()
            wps = tp.ap[:, 0:2048].rearrange("p (k n) -> p k n", k=8)
            wpd = tp.ap[:, 2048:3072].rearrange("p (k n) -> p k n", k=4)
            P.dma("pool", wps, wps_d[:, c2 * 256:(c2 + 1) * 256].rearrange("(k p) n -> p k n", p=128), writes=[tp])
            P.dma("pool", wpd, wpd_d[:, c2 * 256:(c2 + 1) * 256].rearrange("(k p) n -> p k n", p=128), pwrites=[tp])
            for j2 in range(2):
                c = c2 * 2 + j2
                cs = slice(j2 * 128, (j2 + 1) * 128)
                for tg in range(2):
                    ts = slice(tg * 512, (tg + 1) * 512)
                    bA, bB, bC, bD = [next_bank([0, 1, 2, 3, 4, 5, 6, 7]) for _ in range(4)]
                    for k in range(8):
                        mm(ps[bA][:, :], wps[:, k, cs], osb[:, k, ts], k == 0, k == 7, [tp, osbT], bank[bA], k == 7)
                    for k in range(4):
                        mm(ps[bB][:, :], wpd[:, k, cs], odl[:, k, ts], k == 0, k == 3, [tp, odlT], bank[bB], k == 3)
                    for k in range(16):
                        mm(ps[bC][:, :], wgs[:, k, cs], hsl(k, OWN + tg * 512, OWN + (tg + 1) * 512), k == 0, k == 15,
                           [tgs, hTt[2 + tg]], bank[bC], k == 15)
                    for k in range(16):
                        mm(ps[bD][:, :], wgd[:, k, cs], hsl(k, OWN + tg * 512, OWN + (tg + 1) * 512), k == 0, k == 15,
                           [tgd, hTt[2 + tg]], bank[bD], k == 15)
                    sg, sd = eT[0], eT[1]
                    t1, t2 = ecT[0], ecT[1]
                    P.op("act", lambda e, c=c, bC=bC, sg=sg: e.activation(out=sg.ap, in_=ps[bC][:, :], func=AF.Sigmoid,
                                                                       bias=bgT[:, c:c + 1]),
                         reads=[bank[bC], Tc], writes=[sg])
                    P.op("act", lambda e, c=c, bD=bD, sd=sd: e.activation(out=sd.ap, in_=ps[bD][:, :], func=AF.Sigmoid,
                                                                       bias=bgT[:, 16 + c:17 + c]),
                         reads=[bank[bD], Tc], writes=[sd])
                    P.op("dve", lambda e, bA=bA, sg=sg, t1=t1: e.tensor_tensor(out=t1.ap, in0=ps[bA][:, :], in1=sg.ap, op=ALU.mult),
                         reads=[bank[bA], sg], writes=[t1])
                    P.op("dve", lambda e, bB=bB, sd=sd, t2=t2: e.tensor_tensor(out=t2.ap, in0=ps[bB][:, :], in1=sd.ap, op=ALU.mult),
                         reads=[bank[bB], sd], writes=[t2])
                    P.op("dve", lambda e, c=c, ts=ts, t1=t1, t2=t2: e.tensor_tensor(out=mT.ap[:, c, ts], in0=t1.ap, in1=t2.ap, op=ALU.add),
                         reads=[t1, t2], pwrites=[mT])
        if DEBUG:
            P.dma("pool", dbg_mT, mT.ap, reads=[mT], semt=dbgT("dbg4"))

        stop(6)
        xt4 = [T(Mf[:, 16384 + 2048 * i:16384 + 2048 * (i + 1)], "xt4_%d" % i) for i in range(4)]
        xn5 = [T(M[:, 53248 + 4096 * i:53248 + 4096 * (i + 1)], "xn5_%d" % i) for i in range(2)]
        for t in xt4:
            P.handoff(t, [Q2, K2, V2, Q2b, K2b] + xt_t + xn_t)
        for t in xn5:
            P.handoff(t, [osbT])
        for t in h2Tt:
            P.handoff(t, hTt[2:4])
        for qd in range(2):
            for u in range(4):
                tb = qd * 4 + u
                P.dma("sp", xt4[u].ap, x_d[OWN + tb * 128: OWN + (tb + 1) * 128, :], writes=[xt4[u]])
            for cg8 in range(8):
                two, wo_ = wpanel(wout_d[:, cg8 * 256:(cg8 + 1) * 256], 16)
                for u in range(4):
                    tb = qd * 4 + u
                    xt = xt4[u]
                    col = cg8 * 256
                    b = next_bank([0, 1, 2, 3])
                    for k in range(16):
                        mm(ps[b][:, 0:256], mT.ap[:, k, tb * 128:(tb + 1) * 128], wo_[:, k, :],
                           k == 0, k == 15, [mT, two], bank[b], k == 15)
                    tmp = ecT[u % 2]
                    P.op("dve", lambda e, b=b, col=col, tmp=tmp: e.tensor_tensor(out=tmp.ap[:, 0:256], in0=ps[b][:, 0:256],
                                                                              in1=g1b.ap[:, col:col + 256], op=ALU.mult),
                         reads=[bank[b], g1b], writes=[tmp])
                    P.op("dve", lambda e, xt=xt, col=col, tmp=tmp: e.tensor_tensor(out=xt.ap[:, col:col + 256],
                                                                                in0=xt.ap[:, col:col + 256], in1=tmp.ap[:, 0:256],
                                                                                op=ALU.add),
                         reads=[tmp, xt], pwrites=[xt])
            for u in range(4):
                tb = qd * 4 + u
                P.dma("sp", xnew_d[tb * 128:(tb + 1) * 128, :], xt4[u].ap, reads=[xt4[u]], writes=[xnewT[tb]], semt=xnewT[tb])
            for hp_ in range(2):
                pr = qd * 2 + hp_
                norm_T(pr, [xt4[2 * hp_], xt4[2 * hp_ + 1]], h2T, h2Tt[pr // 2], a2t, a2, modT[:, 48:64], modTt, pr * 256,
                       xn=xn5[hp_])
        if DEBUG:
            P.dma("pool", dbg_h2T, h2T, reads=h2Tt, semt=dbgT("dbg5"))

        stop(7)
        P.handoff(actT, [mT, Q2, K2, V2, Q2b, K2b, osbT, odlT] + xt_t + xn_t + xt4 + xn5)
        for p in range(NFF // 2):
            tg_, wg_ = wpanel(wg_d[:, p * 256:(p + 1) * 256], 16)
            tu_, wu_ = wpanel(wu_d[:, p * 256:(p + 1) * 256], 16)
            for j2 in range(2):
                c = p * 2 + j2
                cs = slice(j2 * 128, (j2 + 1) * 128)
                for tg in range(2):
                    ts = slice(tg * 512, (tg + 1) * 512)
                    bG, bU = [next_bank([0, 1, 2, 3, 4, 5, 6, 7]) for _ in range(2)]
                    for k in range(16):
                        mm(ps[bG][:, :], wg_[:, k, cs], h2T[:, k, ts], k == 0, k == 15, [tg_, h2Tt[tg]], bank[bG], k == 15)
                    for k in range(16):
                        mm(ps[bU][:, :], wu_[:, k, cs], h2T[:, k, ts], k == 0, k == 15, [tu_, h2Tt[tg]], bank[bU], k == 15)
                    sg = eT[(c * 2 + tg) % 2]
                    P.op("act", lambda e, bG=bG, sg=sg: e.activation(out=sg.ap, in_=ps[bG][:, :], func=AF.Silu),
                         reads=[bank[bG]], writes=[sg])
                    P.op("dve", lambda e, bU=bU, sg=sg, c=c, ts=ts: e.tensor_tensor(out=actT.ap[:, c, ts], in0=ps[bU][:, :], in1=sg.ap,
                                                                                 op=ALU.mult),
                         reads=[bank[bU], sg], pwrites=[actT])
        stop(8)
        xpT = [T(Mf[:, 512 * i:512 * (i + 1)], "xp%d" % i) for i in range(8)]
        ypT = [T(Mf[:, 4096 + 512 * i:4096 + 512 * (i + 1)], "yp%d" % i) for i in range(8)]
        for t in xpT + ypT:
            P.handoff(t, h2Tt)
        ydT = [T(name="yd%d" % i) for i in range(2)]
        for cg in range(4):
            for tb in range(8):
                P.dma("sp", xpT[tb].ap, xnew_d[tb * 128:(tb + 1) * 128, cg * 512:(cg + 1) * 512], reads=[xnewT[tb]],
                      writes=[xpT[tb]], semt=ydT[0])
            for tb in range(8):
                xpT[tb].w = {ydT[0].dkey: ydT[0].dcnt}
            for c8 in range(6):
                nk = 8 if c8 < 5 else 4
                t = ring_next()
                wd_ = t.ap[:, 0:nk * 512].rearrange("p (k n) -> p k n", k=nk)
                P.dma("pool", wd_, wd_d[c8 * 1024: c8 * 1024 + nk * 128, cg * 512:(cg + 1) * 512].rearrange("(k p) n -> p k n", p=128),
                      writes=[t])
                for kk in range(nk):
                    c = c8 * 8 + kk
                    for tb in range(8):
                        mm(ps[tb][:, :], actT.ap[:, c, tb * 128:(tb + 1) * 128], wd_[:, kk, :], c == 0, c == NFF - 1,
                           [actT, t], bank[tb], c == NFF - 1 or (kk == nk - 1 and tb == 7))
            for tb in range(8):
                xp, yp = xpT[tb], ypT[tb]
                P.op("dve", lambda e, tb=tb, cg=cg, yp=yp: e.tensor_tensor(out=yp.ap, in0=ps[tb][:, :],
                                                                        in1=g2b.ap[:, cg * 512:(cg + 1) * 512], op=ALU.mult),
                     reads=[bank[tb], g2b], writes=[yp])
                P.op("dve", lambda e, xp=xp, yp=yp: e.tensor_tensor(out=yp.ap, in0=yp.ap, in1=xp.ap, op=ALU.add),
                     reads=[xp, yp], pwrites=[yp])
                P.dma("sp", xnew_d[tb * 128:(tb + 1) * 128, cg * 512:(cg + 1) * 512], yp.ap, reads=[yp], pwrites=[xnewT[tb]],
                      semt=ydT[1])
            for tb in range(8):
                ypT[tb].r[ydT[1].dkey] = ydT[1].dcnt
                xnewT[tb].w[ydT[1].dkey] = ydT[1].dcnt

        stop(9)
        zx4 = [T(Mf[:, 2048 * i:2048 * (i + 1)], "zx%d" % i) for i in range(4)]
        zo4 = [T(Mf[:, 8192 + 2048 * i:8192 + 2048 * (i + 1)], "zo%d" % i) for i in range(4)]
        for t in zx4:
            P.handoff(t, h2Tt + xpT + ypT)
        for t in zo4:
            P.handoff(t, [actT, mT])
        P.handoff(gfb, [g1b, accn, accd])
        P.dma("sp", gfb.ap, gfb_d, writes=[gfb])
        for tb in range(8):
            xt = zx4[tb % 4]
            ot = zo4[tb % 4]
            P.dma("pool", xt.ap, xnew_d[tb * 128:(tb + 1) * 128, :], reads=[xnewT[tb]], writes=[xt])
            si = tb
            P.op("act", lambda e, xt=xt, ot=ot, si=si: e.activation(out=ot.ap, in_=xt.ap, func=AF.Square, accum_out=ss[:, si:si + 1]),
                 reads=[xt], writes=[ot], pwrites=[sst])
            P.op("dve", lambda e, si=si: e.tensor_scalar(out=rstd[:, si:si + 1], in0=ss[:, si:si + 1], scalar1=1.0 / D, scalar2=EPS,
                                                         op0=ALU.mult, op1=ALU.add), reads=[sst], pwrites=[rstdt])
            P.op("act", lambda e, si=si: e.activation(out=rstd[:, si:si + 1], in_=rstd[:, si:si + 1], func=AF.Sqrt),
                 reads=[rstdt], pwrites=[rstdt])
            P.op("dve", lambda e, si=si: e.reciprocal(out=rstd[:, si:si + 1], in_=rstd[:, si:si + 1]),
                 reads=[rstdt], pwrites=[rstdt])
            P.op("dve", lambda e, xt=xt, ot=ot, si=si: e.scalar_tensor_tensor(out=ot.ap, in0=xt.ap, scalar=rstd[:, si:si + 1],
                                                                              in1=gfb.ap, op0=ALU.mult, op1=ALU.mult),
                 reads=[xt, rstdt, gfb], writes=[ot])
            P.dma("sp", out_d[tb * 128:(tb + 1) * 128, :], ot.ap, reads=[ot], writes=[outT[tb]], semt=outT[tb])
        P.final_wait("sp", outT)
        if DEBUG:
            P.final_wait("pool", DBGT)

    except StopBuild:
        P.final_wait("sp", DBGT)
        P.final_wait("pool", DBGT)

    with nc.Block() as block:
        @block.tensor
        def _(e):
            for f in P.q["pe"]:
                f(e)

        @block.scalar
        def _(e):
            for f in P.q["act"]:
                f(e)

        @block.vector
        def _(e):
            for f in P.q["dve"]:
                f(e)

        @block.gpsimd
        def _(e):
            for f in P.q["pool"]:
                f(e)

        @block.sync
        def _(e):
            for f in P.q["sp"]:
                f(e)
    es.close()
    build_program.prog = P
    return nc


def host_consts(r):
    v = np.float32(1.0 if r == 1 else 0.0)
    i = np.arange(128)
    c = {}
    c["ident"] = np.eye(128, dtype=np.float32)
    c["negtri"] = -(i[:, None] >= i[None, :]).astype(np.float32)
    c["negrest"] = -(i[:, None] < i[None, :]).astype(np.float32)
    c["dmask"] = (i[None, :] > i[:, None]).astype(np.float32)
    vo = np.ones((128, 3, 128), np.float32)
    vo[:, 0, :] = v
    vo[:64, 2, :] = v
    c["vones"] = vo
    vc = np.ones((128, 3), np.float32)
    vc[:, 0] = v
    vc[:64, 2] = v
    c["vcol"] = vc
    slopes = (2.0 ** (-8.0 * (np.arange(12, dtype=np.float32) + 1.0) / 12)).astype(np.float32)
    dil = [1, 4, 16]
    bt = np.full((128, 12, 256), -30000.0, np.float32)
    b_ = i[:, None]
    a_ = i[None, :]
    for g in range(3):
        for hs in range(4):
            sl = slopes[g * 4 + hs] * dil[g]
            rel = 128 + a_ - b_
            bt[:, g * 4 + hs, 0:128] = np.where(rel <= 128, -sl * rel, -30000.0)
            rel = a_ - b_
            bt[:, g * 4 + hs, 128:256] = np.where(rel >= 0, -sl * rel, -30000.0)
    c["biasT"] = bt.astype(np.float32)
    return c


_NC_CACHE = {}


def kernel(x, c, w_ada, b_ada, g_norm1, g_norm2, g_final, w_in, b_gate, w_proj_sb, w_proj_dil, w_out,
           w_ffn_gate, w_ffn_up, w_ffn_down):
    f = lambda a: np.ascontiguousarray(np.asarray(a, dtype=np.float32))
    x = f(x)
    c = f(c)
    colT = lambda vec, n: np.ascontiguousarray(f(vec).reshape(n, 128).T)
    shared = {
        "w_ada": f(w_ada)[0], "b_ada": f(b_ada)[0], "b_adaT": colT(f(b_ada)[0], 96),
        "g1T": colT(f(g_norm1)[0], 16), "g2T": colT(f(g_norm2)[0], 16),
        "gfinal_b": np.ascontiguousarray(np.broadcast_to(f(g_final)[None, :], (128, D))),
        "w_in": f(w_in)[0], "b_gateT": colT(f(b_gate)[0], 32),
        "w_proj_sb": f(w_proj_sb)[0], "w_proj_dil": f(w_proj_dil)[0], "w_out": f(w_out)[0],
        "w_ffn_gate": f(w_ffn_gate)[0], "w_ffn_up": f(w_ffn_up)[0], "w_ffn_down": f(w_ffn_down)[0],
    }
    consts = [host_consts(0), host_consts(1)]
    in_maps = []
    for core in range(8):
        b, r = core // 2, core % 2
        if r == 1:
            xl = x[b]
        else:
            xl = np.ascontiguousarray(np.concatenate([x[b, OWN:], x[b, :OWN]], axis=0))
        m = dict(shared)
        m.update(consts[r])
        m["x"] = xl
        m["c_col"] = colT(c[b], 16)
        in_maps.append(m)
    if "nc" not in _NC_CACHE:
        _NC_CACHE["nc"] = build_program()
    nc = _NC_CACHE["nc"]
    res = run_bass_kernel_spmd(nc, in_maps, core_ids=list(range(8)))
    out = np.empty((4, S, D), np.float32)
    for core in range(8):
        b, r = core // 2, core % 2
        out[b, r * OWN:(r + 1) * OWN] = res.results[core]["out"]
    if DEBUG:
        kernel.last = res
    return out
```
